# Optimizing a Trainium2 kernel written in Bass

```python
import jax, jax.numpy as jnp
from jax import lax
import numpy as np

D_MODEL = 4096
BATCH = 8
SEQ = 2048
DEPTH = 1
DEC_BATCH = 4
DEC_SEQ = 2048
PAST_LEN = 128

HEAD_DIM = 128
ATTN_WIDTH = D_MODEL // 2
N_HEADS = ATTN_WIDTH // HEAD_DIM
CONV_WIDTH = D_MODEL - ATTN_WIDTH
CONV_TAPS = 31
DILATED_BRANCHES = ((128, 1), (512, 4), (2048, 16))
QBLOCK = 64
ROPE_THETA = 10000.0
D_FF = -(-8 * D_MODEL // (3 * 256)) * 256
IN_COLS = 3 * ATTN_WIDTH + 2 * CONV_WIDTH
EPS = 1e-6
NEG = -1e30

kernel_name = "hybrid_dilated_attn_conformer_conv_encoder"


def _rmsnorm(x, g):
    xf = x.astype(jnp.float32)
    y = xf * lax.rsqrt(jnp.mean(xf * xf, axis=-1, keepdims=True) + EPS)
    return (y * g.astype(jnp.float32)).astype(x.dtype)


def _layernorm(x, g, b):
    xf = x.astype(jnp.float32)
    mu = jnp.mean(xf, axis=-1, keepdims=True)
    xc = xf - mu
    var = jnp.mean(xc * xc, axis=-1, keepdims=True)
    y = xc * lax.rsqrt(var + EPS) * g.astype(jnp.float32) + b.astype(jnp.float32)
    return y.astype(x.dtype)


def _rope(x):
    S = x.shape[1]
    half = HEAD_DIM // 2
    freqs = ROPE_THETA ** (-jnp.arange(half, dtype=jnp.float32) * 2.0 / HEAD_DIM)
    ang = jnp.arange(S, dtype=jnp.float32)[:, None] * freqs[None, :]
    cos = jnp.cos(ang)[None, :, None, :]
    sin = jnp.sin(ang)[None, :, None, :]
    xf = x.astype(jnp.float32)
    x1, x2 = xf[..., :half], xf[..., half:]
    out = jnp.concatenate([x1 * cos - x2 * sin, x1 * sin + x2 * cos], axis=-1)
    return out.astype(x.dtype)


def _dilated_branch(q, k, v, window, dilation):
    B, S, H, Dh = q.shape
    d = dilation
    R = window // (2 * d)
    L = S // d
    nblk = -(-L // QBLOCK)
    Lp = nblk * QBLOCK
    KB = QBLOCK + 2 * R

    def strided(t):
        return t.reshape(B, L, d, H, Dh).transpose(0, 2, 1, 3, 4)

    qs, ks, vs = strided(q), strided(k), strided(v)
    qb = jnp.pad(qs, ((0, 0), (0, 0), (0, Lp - L), (0, 0), (0, 0)))
    qb = qb.reshape(B, d, nblk, QBLOCK, H, Dh)
    kpad = ((0, 0), (0, 0), (R, R + Lp - L), (0, 0), (0, 0))
    idx = jnp.arange(nblk)[:, None] * QBLOCK + jnp.arange(KB)[None, :]
    kb = jnp.pad(ks, kpad)[:, :, idx]
    vb = jnp.pad(vs, kpad)[:, :, idx]

    scale = HEAD_DIM ** -0.5
    s = jnp.einsum('bgnqhd,bgnkhd->bgnhqk', qb, kb,
                   preferred_element_type=jnp.float32) * scale
    qpos = jnp.arange(nblk)[:, None, None] * QBLOCK + jnp.arange(QBLOCK)[None, :, None]
    kpos = jnp.arange(nblk)[:, None, None] * QBLOCK + jnp.arange(KB)[None, None, :] - R
    valid = (jnp.abs(kpos - qpos) <= R) & (kpos >= 0) & (kpos < L)
    s = jnp.where(valid[None, None, :, None, :, :], s, NEG)
    m = jnp.max(s, axis=-1, keepdims=True)
    p = jnp.exp(s - m)
    l = jnp.sum(p, axis=-1)
    o = jnp.einsum('bgnhqk,bgnkhd->bgnhqd', p.astype(v.dtype), vb,
                   preferred_element_type=jnp.float32) / l[..., None]
    lse = m[..., 0] + jnp.log(l)

    o = o.transpose(0, 1, 2, 4, 3, 5).reshape(B, d, Lp, H, Dh)[:, :, :L]
    o = o.transpose(0, 2, 1, 3, 4).reshape(B, S, H, Dh)
    lse = lse.transpose(0, 1, 2, 4, 3).reshape(B, d, Lp, H)[:, :, :L]
    lse = lse.transpose(0, 2, 1, 3).reshape(B, S, H)
    return o, lse


def _mixture_dilated_attention(q, k, v):
    outs, lses = [], []
    for window, dilation in DILATED_BRANCHES:
        o, lse = _dilated_branch(q, k, v, window, dilation)
        outs.append(o)
        lses.append(lse)
    w = jax.nn.softmax(jnp.stack(lses, axis=0), axis=0)
    out = jnp.einsum('rbsh,rbshd->bshd', w, jnp.stack(outs, axis=0))
    return out.astype(q.dtype)


def _conformer_conv(cv, cg, conv_w, conv_b, ln_g, ln_b):
    u = cv * jax.nn.sigmoid(cg)
    C = u.shape[-1]
    pad = (CONV_TAPS - 1) // 2
    y = lax.conv_general_dilated(
        u, conv_w.astype(u.dtype)[:, None, :], window_strides=(1,),
        padding=[(pad, pad)], dimension_numbers=('NWC', 'WIO', 'NWC'),
        feature_group_count=C)
    y = y + conv_b.astype(y.dtype)
    return jax.nn.silu(_layernorm(y, ln_g, ln_b))


def _layer(x, norm_mix_g, w_in, q_norm_g, k_norm_g, conv_w, conv_b, conv_ln_g,
           conv_ln_b, w_out, norm_ffn_g, w_gate, w_up, w_down):
    B, S, _ = x.shape
    hn = _rmsnorm(x, norm_mix_g)
    proj = hn @ w_in
    A, C = ATTN_WIDTH, CONV_WIDTH
    q, k, v, cv, cg = jnp.split(proj, [A, 2 * A, 3 * A, 3 * A + C], axis=-1)
    q = _rope(_rmsnorm(q.reshape(B, S, N_HEADS, HEAD_DIM), q_norm_g))
    k = _rope(_rmsnorm(k.reshape(B, S, N_HEADS, HEAD_DIM), k_norm_g))
    v = v.reshape(B, S, N_HEADS, HEAD_DIM)
    attn = _mixture_dilated_attention(q, k, v).reshape(B, S, A)
    conv = _conformer_conv(cv, cg, conv_w, conv_b, conv_ln_g, conv_ln_b)
    h = x + jnp.concatenate([attn, conv], axis=-1) @ w_out
    hf = _rmsnorm(h, norm_ffn_g)
    ffn = (jax.nn.silu(hf @ w_gate) * (hf @ w_up)) @ w_down
    return h + ffn


def setup_inputs(seed: int = 0) -> dict:
    key = jax.random.key(seed)
    ks = jax.random.split(key, 16)
    f32 = jnp.float32
    nrm = lambda k, shape, s: jax.random.normal(k, shape, f32) * s
    return {
        "x_prompt": nrm(ks[0], (BATCH, SEQ, D_MODEL), 1.0),
        "x_sample": nrm(ks[1], (DEC_BATCH, DEC_SEQ, D_MODEL), 1.0),
        "norm_mix_g": 1.0 + nrm(ks[2], (DEPTH, D_MODEL), 0.02),
        "w_in": nrm(ks[3], (DEPTH, D_MODEL, IN_COLS), D_MODEL ** -0.5),
        "q_norm_g": 1.0 + nrm(ks[4], (DEPTH, HEAD_DIM), 0.02),
        "k_norm_g": 1.0 + nrm(ks[5], (DEPTH, HEAD_DIM), 0.02),
        "conv_w": nrm(ks[6], (DEPTH, CONV_TAPS, CONV_WIDTH), CONV_TAPS ** -0.5),
        "conv_b": nrm(ks[7], (DEPTH, CONV_WIDTH), 0.02),
        "conv_ln_g": 1.0 + nrm(ks[8], (DEPTH, CONV_WIDTH), 0.02),
        "conv_ln_b": nrm(ks[9], (DEPTH, CONV_WIDTH), 0.02),
        "w_out": nrm(ks[10], (DEPTH, D_MODEL, D_MODEL), D_MODEL ** -0.5),
        "norm_ffn_g": 1.0 + nrm(ks[11], (DEPTH, D_MODEL), 0.02),
        "w_gate": nrm(ks[12], (DEPTH, D_MODEL, D_FF), D_MODEL ** -0.5),
        "w_up": nrm(ks[13], (DEPTH, D_MODEL, D_FF), D_MODEL ** -0.5),
        "w_down": nrm(ks[14], (DEPTH, D_FF, D_MODEL), D_FF ** -0.5),
    }


def reference(x_prompt, x_sample, norm_mix_g, w_in, q_norm_g, k_norm_g, conv_w,
              conv_b, conv_ln_g, conv_ln_b, w_out, norm_ffn_g, w_gate, w_up, w_down):
    y_prompt = x_prompt
    y_sample = x_sample
    for l in range(DEPTH):
        p = (norm_mix_g[l], w_in[l], q_norm_g[l], k_norm_g[l], conv_w[l], conv_b[l],
             conv_ln_g[l], conv_ln_b[l], w_out[l], norm_ffn_g[l], w_gate[l],
             w_up[l], w_down[l])
        y_prompt = _layer(y_prompt, *p)
        y_sample = _layer(y_sample, *p)
    return (y_prompt, y_sample)
```

```python
import contextlib
import numpy as np
import concourse.bass as bass
import concourse.mybir as mybir
from concourse.bass_utils import run_bass_kernel_spmd

F32 = mybir.dt.float32
BF16 = mybir.dt.bfloat16
ALU = mybir.AluOpType
AF = mybir.ActivationFunctionType
AX = mybir.AxisListType

D = 4096
S = 2048
NH = 16
HD = 128
AW = 2048
CW = 2048
TAPS = 31
DFF = 11008
NFF = DFF // 128
INC = 10240
EPS = 1e-6
NSLOT = 4
U0 = 1536
MASKW = 3968
DEBUG = False

ENGS = ("pe", "act", "dve", "pool", "sp")
SEM_CHUNK = 30000


class Ins:
    __slots__ = ("eng", "fn", "deps", "dmadeps", "sig", "cnt", "dma_key", "idx")


class Prog:
    def __init__(self):
        self.ins = []
        self.lastw = {}
        self.readers = {}
        self.dma_cnt = {}
        self.per_eng = {e: [] for e in ENGS}
        self.pending = {}

    def op(self, eng, fn, r=(), w=(), dma=None):
        i = Ins()
        i.eng = eng
        i.fn = fn
        i.dma_key = dma
        i.sig = False
        i.cnt = None
        deps = {}
        dmadeps = {}

        def add(d, raw):
            if d.dma_key is not None:
                dmadeps[d.dma_key] = self.dma_cnt[d.dma_key]
                return
            if d.eng == eng:
                if eng == "pe" or not raw:
                    return
            cur = deps.get(d.eng)
            if cur is None or d.idx > cur.idx:
                deps[d.eng] = d

        for k in r:
            lw = self.lastw.get(k)
            if lw is not None:
                add(lw, True)
        for k in w:
            lw = self.lastw.get(k)
            if lw is not None:
                add(lw, False)
            for rd in self.readers.get(k, {}).values():
                add(rd, False)
        pend = self.pending.pop(eng, None)
        if pend is not None:
            for d in pend[0]:
                if d.eng != eng:
                    cur = deps.get(d.eng)
                    if cur is None or d.idx > cur.idx:
                        deps[d.eng] = d
            for k, v in pend[1].items():
                dmadeps[k] = max(dmadeps.get(k, 0), v)
        i.deps = list(deps.values())
        i.dmadeps = dmadeps
        i.idx = len(self.ins)
        self.ins.append(i)
        self.per_eng[eng].append(i)
        if dma is not None:
            self.dma_cnt[dma] = self.dma_cnt.get(dma, 0) + 1
        rk = ("dma", dma) if dma is not None else eng
        for k in r:
            self.readers.setdefault(k, {})[rk] = i
        for k in w:
            self.lastw[k] = i
            self.readers[k] = {}
        return i

    def barrier(self):
        lasts = []
        for e in ENGS:
            for i in reversed(self.per_eng[e]):
                if i.dma_key is None:
                    lasts.append(i)
                    break
        snap = dict(self.dma_cnt)
        for e in ENGS:
            self.pending[e] = (list(lasts), dict(snap))

    def emit(self, nc, es, final_wait_keys=()):
        for i in self.ins:
            for d in i.deps:
                d.sig = True
        nsig = {}
        for e in ENGS:
            c = 0
            for i in self.per_eng[e]:
                if i.sig:
                    c += 1
                    i.cnt = c
            nsig[e] = c
        esem = {}
        for e in ENGS:
            n = max(1, -(-nsig[e] // SEM_CHUNK))
            esem[e] = [es.enter_context(nc.semaphore(f"s_{e}_{j}")) for j in range(n)]
        dsem = {}
        for n_, k in enumerate(self.dma_cnt):
            dsem[k] = es.enter_context(nc.semaphore(f"d_{n_}"))
        block = es.enter_context(nc.Block())
        prog = self

        def run_engine(e, h):
            waited = {}
            for i in prog.per_eng[e]:
                for d in i.deps:
                    c = d.cnt - 1
                    ch = c // SEM_CHUNK
                    v = c % SEM_CHUNK + 1
                    done = False
                    for (de, ch2), wv in waited.items():
                        if de == d.eng and (ch2 > ch or (ch2 == ch and wv >= v)):
                            done = True
                            break
                    if not done:
                        h.wait_ge(esem[d.eng][ch], v)
                        waited[(d.eng, ch)] = v
                for k, cntv in i.dmadeps.items():
                    kk = ("dma", k)
                    v = 16 * cntv
                    if waited.get(kk, 0) < v:
                        h.wait_ge(dsem[k], v)
                        waited[kk] = v
                bi = i.fn(h)
                if i.dma_key is not None:
                    bi.then_inc(dsem[i.dma_key], 16)
                elif i.sig:
                    c = i.cnt - 1
                    bi.then_inc(esem[e][c // SEM_CHUNK], 1)
            if e == "sp":
                for k in final_wait_keys:
                    h.wait_ge(dsem[k], 16 * prog.dma_cnt[k])

        @block.tensor
        def _(h):
            run_engine("pe", h)

        @block.scalar
        def _(h):
            run_engine("act", h)

        @block.vector
        def _(h):
            run_engine("dve", h)

        @block.gpsimd
        def _(h):
            run_engine("pool", h)

        @block.sync
        def _(h):
            run_engine("sp", h)


class Arena:
    def __init__(self, tensor, nbytes):
        self.t = tensor
        self.n = nbytes
        self.off = 0

    def alloc(self, shape, dt):
        esz = 4 if dt == F32 else 2
        free = 1
        for s_ in shape[1:]:
            free *= s_
        nb = free * esz
        self.off = (self.off + 31) // 32 * 32
        assert self.off + nb <= self.n, ("SBUF arena overflow", self.off, nb, self.n)
        a = self.t[:, self.off // 2:(self.off + nb) // 2]
        self.off += nb
        if dt == F32:
            a = a.bitcast(F32)
        if len(shape) == 3:
            a = a.rearrange("p (a b) -> p a b", b=shape[2])
        return a

    def mark(self):
        return self.off

    def reset(self, m):
        self.off = m


def bc(ap, shape):
    return ap.broadcast_to(shape)


def build_program():
    nc = bass.Bass("TRN2", target_bir_lowering=False)
    dt_in = lambda name, shape: nc.dram_tensor(name, shape, F32, kind="ExternalInput").ap()
    xs_d = [dt_in("xa", [S, D]), dt_in("xb", [S, D])]
    w_in = dt_in("w_in", [D, INC])
    w_out = dt_in("w_out", [D, D])
    w_gate = dt_in("w_gate", [D, DFF])
    w_up = dt_in("w_up", [D, DFF])
    w_down = dt_in("w_down", [DFF, D])
    gmix_d = dt_in("gmix", [128, 32])
    gffn_d = dt_in("gffn", [128, 32])
    gq_d = dt_in("gq", [128, 128])
    gk_d = dt_in("gk", [128, 128])
    cw_d = [dt_in("cwa", [128, 16, TAPS]), dt_in("cwb", [128, 16, TAPS])]
    cb_d = dt_in("cb", [128, 16])
    lg_d = dt_in("lg", [128, 16])
    lb_d = dt_in("lb", [128, 16])
    cos_d = [dt_in("cosa", [128, 16, 64]), dt_in("cosb", [128, 16, 64])]
    sin_d = [dt_in("sina", [128, 16, 64]), dt_in("sinb", [128, 16, 64])]
    mask_d = dt_in("maskm", [128, MASKW])
    ys_d = [nc.dram_tensor("ya", [S, D], F32, kind="ExternalOutput").ap(),
            nc.dram_tensor("yb", [1024, D], F32, kind="ExternalOutput").ap()]
    skind = "ExternalOutput" if DEBUG else "Internal"
    TQ = [2048, 1024]
    UW = [15 + 2048 + 15, 15 + 1152]
    qT_s = [nc.dram_tensor(f"qT_s{s}", [NH, 128, TQ[s]], BF16, kind=skind).ap() for s in range(2)]
    kT_s = [nc.dram_tensor(f"kT_s{s}", [NH, 128, S], BF16, kind=skind).ap() for s in range(2)]
    v_s = [nc.dram_tensor(f"v_s{s}", [S, AW], BF16, kind=skind).ap() for s in range(2)]
    uT_s = [nc.dram_tensor(f"uT_s{s}", [CW, UW[s]], F32, kind=skind).ap() for s in range(2)]
    attnT_s = [nc.dram_tensor(f"attnT_s{s}", [AW, TQ[s]], BF16, kind=skind).ap() for s in range(2)]
    h_s = nc.dram_tensor("h_s", [6, 512, D], F32, kind=skind).ap()
    y_s = nc.dram_tensor("y_s", [6, CW, 512], F32, kind=skind).ap()

    w_in_v = w_in.rearrange("(kc p) n -> p kc n", p=128)
    w_out_v = w_out.rearrange("(kc p) n -> p kc n", p=128)
    w_gate_v = w_gate.rearrange("(kc p) n -> p kc n", p=128)
    w_up_v = w_up.rearrange("(kc p) n -> p kc n", p=128)
    w_down_v = w_down.rearrange("(kc p) n -> p kc n", p=128)

    P = Prog()
    es = contextlib.ExitStack()
    with es:
        ARENA_BYTES = 207 * 1024
        arena_t = es.enter_context(nc.sbuf_tensor("arena", [128, ARENA_BYTES // 2], BF16))
        AR = Arena(arena_t, ARENA_BYTES)
        ps = [es.enter_context(nc.psum_tensor(f"ps{i}", [128, 512], F32)) for i in range(8)]
        psb = [p_[:].bitcast(BF16) for p_ in ps]
        st = {"bank": 0, "slot": 0, "n": 0}

        def next_bank():
            b = st["bank"]
            st["bank"] = (b + 1) % 8
            return b

        def uid():
            st["n"] += 1
            return st["n"]

        ident = AR.alloc([128, 128], BF16)
        identf = AR.alloc([128, 128], F32)
        onesf = AR.alloc([128, 128], F32)
        onesb = AR.alloc([128, 128], BF16)
        gmix = AR.alloc([128, 32], F32)
        gffn = AR.alloc([128, 32], F32)
        gq = AR.alloc([128, 128], F32)
        gk = AR.alloc([128, 128], F32)
        cw = [AR.alloc([128, 16, TAPS], F32) for _ in range(2)]
        cb = AR.alloc([128, 16], F32)
        lg = AR.alloc([128, 16], F32)
        lb = AR.alloc([128, 16], F32)
        epsb = AR.alloc([128, 1], F32)
        negB = AR.alloc([128, 1], F32)
        sm = AR.alloc([128, 8], F32)
        wslots = [AR.alloc([128, 4096], BF16) for _ in range(NSLOT)]
        s1acc = AR.alloc([128, 512], F32)
        s2acc = AR.alloc([128, 512], F32)
        lnr = AR.alloc([128, 512], F32)
        lnb = AR.alloc([128, 512], F32)
        tiles3 = [(0, t0) for t0 in (0, 512, 1024, 1536)] + [(1, 0), (1, 512)]
        y_keys = {}

        def conv_gen(tidx, acc, uwin, ysq, bankfn, pfx, immediate=False):
            seq, t0 = tiles3[tidx]
            cwt = cw[seq]
            cwk = "cw%d" % seq
            nu = len(uwin) // 2
            prev_stats = None
            for p in range(8):
                chains = []
                for n_, c in enumerate((2 * p, 2 * p + 1)):
                    ui = (p % nu) * 2 + n_
                    i_ = P.op("sp", lambda h, ui=ui, c=c: h.dma_start(out=uwin[ui][:, 0:542], in_=uT_s[seq][c * 128:(c + 1) * 128, t0:t0 + 542]),
                              r=[("uT", seq)], w=[(pfx + "uwin", ui)], dma=(pfx + "uwin", ui))
                    for k_ in list(P.dma_cnt):
                        if k_ == "uT_st" or (isinstance(k_, tuple) and k_[0] == "uT_st"):
                            i_.dmadeps[k_] = P.dma_cnt[k_]
                    chains.append((c, ui, 2 * n_))
                if prev_stats is not None:
                    prev_stats()
                    prev_stats = None
                for (c, ui, a0) in chains:
                    P.op("dve", lambda h, c=c, ui=ui, a0=a0: h.tensor_scalar(out=acc[a0], in0=uwin[ui][:, 0:512], scalar1=cwt[:, c, 0:1], scalar2=cb[:, c:c + 1], op0=ALU.mult, op1=ALU.add),
                         r=[(pfx + "uwin", ui), cwk, "cb"], w=[(pfx + "acc", a0)])
                yield None
                cur = 0
                for tap in range(1, TAPS):
                    nxt = 1 - cur
                    for (c, ui, a0) in chains:
                        P.op("dve", lambda h, tap=tap, cur=cur, nxt=nxt, c=c, ui=ui, a0=a0: h.scalar_tensor_tensor(
                            out=acc[a0 + nxt], in0=uwin[ui][:, tap:tap + 512], scalar=cwt[:, c, tap:tap + 1], in1=acc[a0 + cur], op0=ALU.mult, op1=ALU.add),
                            r=[(pfx + "uwin", ui), cwk, (pfx + "acc", a0 + cur)], w=[(pfx + "acc", a0 + nxt)])
                    cur = nxt
                    if tap < TAPS - 1:
                        yield None
                fins = []
                for n_, (c, ui, a0) in enumerate(chains):
                    fin = a0 + cur
                    fins.append((n_, c, fin))
                    P.op("act", lambda h, fin=fin, n_=n_: h.activation(out=ysq[n_], in_=acc[fin], func=AF.Square), r=[(pfx + "acc", fin)], w=[(pfx + "ysq", n_)])
                    P.op("sp", lambda h, fin=fin, c=c: h.dma_start(out=y_s[tidx, c * 128:(c + 1) * 128, :], in_=acc[fin]), r=[(pfx + "acc", fin)], w=[("y_s", tidx)], dma="yst")

                def stats(fins=fins):
                    for (n_, c, fin) in fins:
                        for (src, skey, dst, dkey) in ((acc[fin], (pfx + "acc", fin), s1acc, "s1acc"), (ysq[n_], (pfx + "ysq", n_), s2acc, "s2acc")):
                            b = bankfn()
                            mm(ps[b][:], onesf, src, True, True, ["onesf", skey], [("ps", b)])
                            if c == 0:
                                P.op("dve", lambda h, b=b, dst=dst: h.tensor_copy(out=dst, in_=ps[b][:]), r=[("ps", b)], w=[dkey])
                            else:
                                P.op("dve", lambda h, b=b, dst=dst: h.tensor_tensor(out=dst, in0=ps[b][:], in1=dst, op=ALU.add), r=[("ps", b), dkey], w=[dkey])
                if immediate:
                    stats()
                    yield "P"
                else:
                    prev_stats = stats
                    yield "B"
            if prev_stats is not None:
                prev_stats()
            yield "B"

        def drive(gen, n=8):
            if gen is None:
                return
            for _ in range(n):
                try:
                    if next(gen) == "B":
                        break
                except StopIteration:
                    break

        def run_pairs(gen, n):
            seen = 0
            while seen < n:
                try:
                    if next(gen) == "P":
                        seen += 1
                except StopIteration:
                    break

        def drain(gen):
            if gen is None:
                return
            for _ in gen:
                pass

        def ln_finalize(tmp0, tmp1, k0, k1):
            P.op("dve", lambda h: h.tensor_single_scalar(out=s1acc, in_=s1acc, scalar=1.0 / CW, op=ALU.mult), r=["s1acc"], w=["s1acc"])
            P.op("dve", lambda h: h.tensor_tensor(out=tmp0, in0=s1acc, in1=s1acc, op=ALU.mult), r=["s1acc"], w=[k0])
            P.op("dve", lambda h: h.scalar_tensor_tensor(out=tmp1, in0=s2acc, scalar=1.0 / CW, in1=tmp0, op0=ALU.mult, op1=ALU.subtract),
                 r=["s2acc", k0], w=[k1])
            P.op("act", lambda h: h.activation(out=tmp0, in_=tmp1, func=AF.Sqrt, bias=epsb), r=[k1, "epsb"], w=[k0])
            P.op("dve", lambda h: h.reciprocal(out=lnr, in_=tmp0), r=[k0], w=["lnr"])
            P.op("dve", lambda h: h.scalar_tensor_tensor(out=lnb, in0=s1acc, scalar=-1.0, in1=lnr, op0=ALU.mult, op1=ALU.mult), r=["s1acc", "lnr"], w=["lnb"])

        def ld(dst, src, key):
            P.op("sp", lambda h: h.dma_start(out=dst, in_=src), w=[key], dma="const")

        ld(gmix, gmix_d, "gmix"); ld(gffn, gffn_d, "gffn"); ld(gq, gq_d, "gq"); ld(gk, gk_d, "gk")
        ld(cw[0], cw_d[0], "cw0"); ld(cw[1], cw_d[1], "cw1"); ld(cb, cb_d, "cb"); ld(lg, lg_d, "lg"); ld(lb, lb_d, "lb")
        P.op("dve", lambda h: h.memset(identf, 0.0), w=["identf"])
        P.op("dve", lambda h: h.memset(onesf, 1.0), w=["onesf"])
        P.op("dve", lambda h: h.memset(epsb, EPS), w=["epsb"])
        P.op("pool", lambda h: h.affine_select(out=identf, in_=onesf, pattern=[[-1, 128]], compare_op=ALU.is_equal,
                                               fill=0.0, base=0, channel_multiplier=1), r=["onesf"], w=["identf"])
        P.op("dve", lambda h: h.tensor_copy(out=ident, in_=identf), r=["identf"], w=["ident"])
        P.op("dve", lambda h: h.tensor_copy(out=onesb, in_=onesf), r=["onesf"], w=["onesb"])
        P.op("dve", lambda h: h.tensor_reduce(out=sm[:, 0:1], in_=gq, axis=AX.X, op=ALU.max, apply_absolute_value=True), r=["gq"], w=["sm0"])
        P.op("dve", lambda h: h.tensor_reduce(out=sm[:, 1:2], in_=gk, axis=AX.X, op=ALU.max, apply_absolute_value=True), r=["gk"], w=["sm1"])
        P.op("dve", lambda h: h.tensor_tensor(out=sm[:, 2:3], in0=sm[:, 0:1], in1=sm[:, 1:2], op=ALU.mult), r=["sm0", "sm1"], w=["sm2"])
        P.op("dve", lambda h: h.tensor_single_scalar(out=negB, in_=sm[:, 2:3], scalar=-(128.0 ** 0.5), op=ALU.mult), r=["sm2"], w=["negB"])
        zt = AR.alloc([128, 16, 15], F32)
        P.op("dve", lambda h: h.memset(zt, 0.0), w=["zt"])
        for s in range(2):
            P.op("sp", lambda h, s=s: h.dma_start(out=uT_s[s][:, 0:15].rearrange("(c p) z -> p c z", p=128), in_=zt), r=["zt"], w=[("uT", s)], dma="uT_st")
        P.op("sp", lambda h: h.dma_start(out=uT_s[0][:, 15 + 2048:15 + 2048 + 15].rearrange("(c p) z -> p c z", p=128), in_=zt), r=["zt"], w=[("uT", 0)], dma="uT_st")

        deferred = []

        def wload(dram_ap, nk, ncols):
            i = st["slot"] % NSLOT
            st["slot"] += 1
            view = wslots[i][:, 0:nk * ncols].rearrange("p (k n) -> p k n", n=ncols)
            P.op("pool", lambda h: h.dma_start(out=view, in_=dram_ap), w=[("w", i)], dma=("w", i))
            return view, ("w", i)

        def mm(out, lhsT, rhs, start, stop, r, w):
            P.op("pe", lambda h: h.matmul(out, lhsT=lhsT, rhs=rhs, start=start, stop=stop), r=r, w=w)

        def tm_group(wblocks, lhs, lhs_keys, epi, nts=4, epi_all=None):
            banks = [next_bank() for _ in range(nts)]
            nkt = sum(nk for _, nk in wblocks)
            kg0 = 0
            for dap, nk in wblocks:
                view, wkey = wload(dap, nk, 512)
                for ts in range(nts):
                    for k in range(nk):
                        kg = kg0 + k
                        mm(ps[banks[ts]][:], lhs(kg, ts), view[:, k, :], kg == 0, kg == nkt - 1,
                           [wkey] + lhs_keys, [("ps", banks[ts])])
                kg0 += nk
            while deferred:
                deferred.pop(0)()
            if epi_all is not None:
                epi_all(banks)
            else:
                for ts in range(nts):
                    epi(ts, banks[ts])

        def fm_pair(wA, wB, rhs, rhs_keys, NT, epi):
            bA = [next_bank(), next_bank()]
            bB = [next_bank(), next_bank()]
            for kb2 in range(2):
                for blocks, banks in ((wA, bA), (wB, bB)):
                    view, wkey = wload(blocks[kb2], 16, 256)
                    for j in range(2):
                        for k in range(16):
                            kg = kb2 * 16 + k
                            mm(ps[banks[j]][:, 0:NT], view[:, k, j * 128:(j + 1) * 128], rhs(kg), kg == 0, kg == 31,
                               [wkey] + rhs_keys, [("ps", banks[j])])
            while deferred:
                deferred.pop(0)()
            for j in range(2):
                epi(j, bA[j], bB[j])

        def transposes_to(src_fn, src_keys, dst, dst_key, ts, gcol):
            for g in range(4):
                b = next_bank()
                for k in range(8):
                    kc = g * 8 + k
                    P.op("pe", lambda h, b=b, k=k, kc=kc: h.transpose(psb[b][:, k * 128:(k + 1) * 128], src_fn(kc), ident),
                         r=src_keys + ["ident"], w=[("ps", b)])
                P.op("dve", lambda h, b=b, g=g: h.tensor_tensor(
                    out=dst[:, g * 8:(g + 1) * 8, ts * 128:(ts + 1) * 128],
                    in0=psb[b].rearrange("p (k t) -> p k t", t=128),
                    in1=bc(gcol[:, g * 8:(g + 1) * 8].unsqueeze(2), [128, 8, 128]), op=ALU.mult),
                    r=[("ps", b), "gmix", "gffn"], w=[dst_key])

        base_mark = AR.mark()

        hnT = [AR.alloc([128, 32, 512], BF16) for _ in range(2)]
        xst = [AR.alloc([128, D], F32)] * 2
        hn_tm = [AR.alloc([128, D], BF16)] * 2
        cos_t = AR.alloc([128, 4, 64], F32)
        sin_t = AR.alloc([128, 4, 64], F32)
        rtab = [[AR.alloc([128, 4, 64], F32) for _ in range(4)] for _ in range(2)]
        sqt = [AR.alloc([128, 512], F32) for _ in range(4)]
        qn = [AR.alloc([128, 4, 128], F32) for _ in range(4)]
        rt = [AR.alloc([128, 4, 64], F32) for _ in range(8)]
        qo = [AR.alloc([128, 4, 128], BF16) for _ in range(4)]
        ssq = [AR.alloc([128, 8], F32) for _ in range(4)]
        qstage = [AR.alloc([128, 4, 512], BF16)] * 2
        vst = [AR.alloc([128, 512], BF16) for _ in range(2)]
        sg = [AR.alloc([128, 512], F32)] * 2
        ust = [AR.alloc([128, 512], F32) for _ in range(2)]
        acc1 = [AR.alloc([128, 512], F32) for _ in range(4)]
        uwin1 = [AR.alloc([128, 544], F32) for _ in range(2)]
        ysq1 = [AR.alloc([128, 512], F32) for _ in range(2)]
        print("phase1 arena", AR.off)
        rst1 = [AR.alloc([128, 2], F32) for _ in range(2)]
        tiles1 = [dict(seq=0, t0=t0, q=True, conv=512) for t0 in (0, 512, 1024, 1536)]
        tiles1 += [dict(seq=1, t0=0, q=True, conv=512), dict(seq=1, t0=512, q=True, conv=512),
                   dict(seq=1, t0=1024, q=False, conv=128), dict(seq=1, t0=1536, q=False, conv=0)]
        cnt1 = {"x": 0, "u": 0, "v": 0, "qk": 0}

        def prep_tables(ti):
            tl = tiles1[ti]
            blk0 = tl["t0"] // 128
            sq_ = tl["seq"]
            P.op("sp", lambda h: h.dma_start(out=cos_t, in_=cos_d[sq_][:, blk0:blk0 + 4, :]), w=["cos_t"], dma="rope_ld")
            P.op("sp", lambda h: h.dma_start(out=sin_t, in_=sin_d[sq_][:, blk0:blk0 + 4, :]), w=["sin_t"], dma="rope_ld")
            for qk_i, (gv, gkey) in enumerate(((gq, "gq"), (gk, "gk"))):
                g1 = bc(gv[:, 0:64].unsqueeze(1), [128, 4, 64])
                g2 = bc(gv[:, 64:128].unsqueeze(1), [128, 4, 64])
                for n_, (tab, gg) in enumerate(((cos_t, g1), (sin_t, g2), (sin_t, g1), (cos_t, g2))):
                    P.op("dve", lambda h, tab=tab, gg=gg, dst=rtab[qk_i][n_]: h.tensor_tensor(out=dst, in0=tab, in1=gg, op=ALU.mult),
                         r=["cos_t", "sin_t", gkey], w=[("rtab", qk_i)])

        def prep_norm(ti, ts):
            tl = tiles1[ti]
            xb_ = cnt1["x"] % 2
            cnt1["x"] += 1
            r0 = tl["t0"] + ts * 128
            P.op("sp", lambda h, xb_=xb_, r0=r0, s=tl["seq"]: h.dma_start(out=xst[xb_], in_=xs_d[s][r0:r0 + 128, :]),
                 w=[("xst", 0)], dma=("xst", 0))
            P.op("dve", lambda h, xb_=xb_: h.memset(rst1[xb_][:, 0:1], 0.0), w=[("rst1a", xb_)])
            P.op("act", lambda h, xb_=xb_: h.activation(out=hn_tm[xb_], in_=xst[xb_], func=AF.Square, accum_out=rst1[xb_][:, 0:1]),
                 r=[("xst", 0)], w=[("hn_tm", 0), ("rst1a", xb_)])
            P.op("act", lambda h, xb_=xb_: h.activation(out=rst1[xb_][:, 1:2], in_=rst1[xb_][:, 0:1], func=AF.Sqrt, scale=1.0 / D, bias=epsb),
                 r=[("rst1a", xb_), "epsb"], w=[("rst1b", xb_)])
            P.op("dve", lambda h, xb_=xb_: h.reciprocal(out=rst1[xb_][:, 0:1], in_=rst1[xb_][:, 1:2]),
                 r=[("rst1b", xb_)], w=[("rst1a", xb_)])
            P.op("act", lambda h, xb_=xb_: h.activation(out=hn_tm[xb_], in_=xst[xb_], func=AF.Copy, scale=rst1[xb_][:, 0:1]),
                 r=[("xst", 0), ("rst1a", xb_)], w=[("hn_tm", 0)])

        def prep_tr(ti, ts):
            hb = ti % 2
            transposes_to(lambda kc: hn_tm[0][:, kc * 128:(kc + 1) * 128], [("hn_tm", 0)],
                          hnT[hb], ("hnT", hb), ts, gmix)

        def prep1(ti):
            prep_tables(ti)
            for ts in range(4):
                prep_norm(ti, ts)
                prep_tr(ti, ts)

        def qk_epi(tl, hb, cgi, is_q):
            qk_i = 0 if is_q else 1
            dst = (qT_s if is_q else kT_s)[tl["seq"]]
            h0 = (cgi % 4) * 4
            sb_ = cnt1["qk"] % 2
            cnt1["qk"] += 1
            stage = qstage[sb_]
            skey = ("qstage", 0)
            seq = tl["seq"]
            C1, S2, S1, C2 = rtab[qk_i]
            tk = ("rtab", qk_i)

            def epi_all(banks):
                R = range(4)
                for ts in R:
                    if ts < 2:
                        P.op("act", lambda h, ts=ts: h.activation(out=qn[ts].rearrange("p a d -> p (a d)"), in_=ps[banks[ts]][:], func=AF.Copy),
                             r=[("ps", banks[ts])], w=[("qn", ts)])
                    else:
                        P.op("dve", lambda h, ts=ts: h.tensor_copy(out=qn[ts].rearrange("p a d -> p (a d)"), in_=ps[banks[ts]][:]),
                             r=[("ps", banks[ts])], w=[("qn", ts)])
                for ts in R:
                    P.op("act", lambda h, ts=ts: h.activation(out=sqt[ts], in_=qn[ts].rearrange("p a d -> p (a d)"), func=AF.Square), r=[("qn", ts)], w=[("sqt", ts)])
                for ts in R:
                    P.op("dve", lambda h, ts=ts: h.tensor_reduce(out=ssq[ts][:, 0:4], in_=sqt[ts].rearrange("p (a d) -> p a d", d=128), axis=AX.X, op=ALU.add),
                         r=[("sqt", ts)], w=[("ssq", ts)])
                for ts in R:
                    P.op("act", lambda h, ts=ts: h.activation(out=ssq[ts][:, 4:8], in_=ssq[ts][:, 0:4], func=AF.Sqrt, scale=1.0 / HD, bias=epsb),
                         r=[("ssq", ts), "epsb"], w=[("ssqb", ts)])
                for ts in R:
                    P.op("dve", lambda h, ts=ts: h.reciprocal(out=ssq[ts][:, 0:4], in_=ssq[ts][:, 4:8]), r=[("ssqb", ts)], w=[("ssq", ts)])
                for ts in R:
                    P.op("dve", lambda h, ts=ts: h.tensor_tensor(out=qn[ts], in0=qn[ts],
                                                                 in1=bc(ssq[ts][:, 0:4].unsqueeze(2), [128, 4, 128]), op=ALU.mult),
                         r=[("qn", ts), ("ssq", ts)], w=[("qn", ts)])
                for (ta, tb_, half, op_) in ((C1, S2, 0, ALU.subtract), (S1, C2, 1, ALU.add)):
                    for ts in R:
                        P.op("dve", lambda h, ts=ts, ta=ta: h.tensor_tensor(out=rt[2 * ts], in0=qn[ts][:, :, 0:64], in1=bc(ta[:, ts, :].unsqueeze(1), [128, 4, 64]), op=ALU.mult),
                             r=[("qn", ts), tk], w=[("rt", 2 * ts)])
                    for ts in R:
                        P.op("dve", lambda h, ts=ts, tb_=tb_: h.tensor_tensor(out=rt[2 * ts + 1], in0=qn[ts][:, :, 64:128], in1=bc(tb_[:, ts, :].unsqueeze(1), [128, 4, 64]), op=ALU.mult),
                             r=[("qn", ts), tk], w=[("rt", 2 * ts + 1)])
                    for ts in R:
                        P.op("dve", lambda h, ts=ts, half=half, op_=op_: h.tensor_tensor(out=qo[ts][:, :, half * 64:(half + 1) * 64], in0=rt[2 * ts], in1=rt[2 * ts + 1], op=op_),
                             r=[("rt", 2 * ts), ("rt", 2 * ts + 1)], w=[("qo", ts)])

                def late():
                    tbs = [next_bank(), next_bank()]
                    for ts in R:
                        tb = tbs[ts // 2]
                        o0 = (ts % 2) * 512
                        for a_ in range(4):
                            P.op("pe", lambda h, a_=a_, ts=ts, tb=tb, o0=o0: h.transpose(psb[tb][:, o0 + a_ * 128:o0 + (a_ + 1) * 128], qo[ts][:, a_, :], ident),
                                 r=[("qo", ts), "ident"], w=[("ps", tb)])
                    for ts in R:
                        tb = tbs[ts // 2]
                        o0 = (ts % 2) * 512
                        eng_ = "act" if ts % 2 == 0 else "dve"
                        if eng_ == "act":
                            P.op("act", lambda h, ts=ts, tb=tb, o0=o0: h.activation(out=stage[:, :, ts * 128:(ts + 1) * 128],
                                                                                    in_=psb[tb][:, o0:o0 + 512].rearrange("p (a t) -> p a t", t=128), func=AF.Copy),
                                 r=[("ps", tb)], w=[skey])
                        else:
                            P.op("dve", lambda h, ts=ts, tb=tb, o0=o0: h.tensor_copy(out=stage[:, :, ts * 128:(ts + 1) * 128],
                                                                                     in_=psb[tb][:, o0:o0 + 512].rearrange("p (a t) -> p a t", t=128)),
                                 r=[("ps", tb)], w=[skey])
                    t0 = tl["t0"]
                    P.op("sp", lambda h: h.dma_start(out=dst[h0:h0 + 4, :, t0:t0 + 512].rearrange("a d t -> d a t"), in_=stage),
                         r=[skey], w=[("qk_s", seq)], dma=("qk_st", sb_))
                deferred.append(late)
            return epi_all

        def v_epi(tl, cgi):
            seq = tl["seq"]

            def epi(ts, b):
                i3 = cnt1["v"] % 2
                cnt1["v"] += 1
                r0 = tl["t0"] + ts * 128
                c0 = (cgi - 8) * 512
                P.op("act", lambda h: h.activation(out=vst[i3], in_=ps[b][:], func=AF.Copy), r=[("ps", b)], w=[("vst", i3)])
                P.op("sp", lambda h: h.dma_start(out=v_s[seq][r0:r0 + 128, c0:c0 + 512], in_=vst[i3]), r=[("vst", i3)], w=[("v_s", seq)], dma=("v_st", i3))
            return epi

        def glu_epi(tl, gi, NT):
            seq = tl["seq"]

            def epi(j, bA_, bB_):
                i2 = uid() % 2
                i3 = cnt1["u"] % 2
                cnt1["u"] += 1
                c = gi * 2 + j
                c0 = 15 + tl["t0"]
                P.op("act", lambda h: h.activation(out=sg[i2][:, 0:NT], in_=ps[bB_][:, 0:NT], func=AF.Sigmoid), r=[("ps", bB_)], w=[("sg", 0)])
                P.op("dve", lambda h: h.tensor_tensor(out=ust[i3][:, 0:NT], in0=ps[bA_][:, 0:NT], in1=sg[i2][:, 0:NT], op=ALU.mult),
                     r=[("ps", bA_), ("sg", 0)], w=[("ust", i3)])
                P.op("sp", lambda h: h.dma_start(out=uT_s[seq][c * 128:(c + 1) * 128, c0:c0 + NT], in_=ust[i3][:, 0:NT]),
                     r=[("ust", i3)], w=[("uT", seq)], dma=("uT_st", i3))
            return epi

        prep1(0)
        gen0 = conv_gen(0, acc1, uwin1, ysq1, next_bank, "p1")
        for ti, tl in enumerate(tiles1):
            hb = ti % 2
            groups = []
            for cgi in range(12):
                if cgi < 4 and not tl["q"]:
                    continue
                groups.append(("tm", cgi))
            if tl["conv"]:
                for gi in range(8):
                    groups.append(("fm", gi))
            lhs = lambda kg, ts, hb=hb: hnT[hb][:, kg, ts * 128:(ts + 1) * 128]
            for gidx, (kind, gi) in enumerate(groups):
                G_ = len(groups)
                if ti + 1 < len(tiles1):
                    k_ = gidx - (G_ - 5)
                    if k_ == 0:
                        prep_tables(ti + 1)
                    if 1 <= k_ <= 4:
                        prep_tr(ti + 1, k_ - 1)
                    if 0 <= k_ <= 3:
                        prep_norm(ti + 1, k_)
                if ti >= 2 and not (kind == "tm" and gi < 8):
                    drive(gen0, 8)
                if kind == "tm":
                    wb = [(w_in_v[:, kb * 8:(kb + 1) * 8, gi * 512:(gi + 1) * 512], 8) for kb in range(4)]
                    if gi < 8:
                        tm_group(wb, lhs, [("hnT", hb)], None, epi_all=qk_epi(tl, hb, gi, gi < 4))
                    else:
                        tm_group(wb, lhs, [("hnT", hb)], v_epi(tl, gi))
                else:
                    NT = tl["conv"]
                    ca = 3 * AW + gi * 256
                    cg_ = 3 * AW + CW + gi * 256
                    wA = [w_in_v[:, kb * 16:(kb + 1) * 16, ca:ca + 256] for kb in range(2)]
                    wB = [w_in_v[:, kb * 16:(kb + 1) * 16, cg_:cg_ + 256] for kb in range(2)]
                    fm_pair(wA, wB, lambda kg, hb=hb, NT=NT: hnT[hb][:, kg, 0:NT], [("hnT", hb)], NT, glu_epi(tl, gi, NT))
            while deferred:
                deferred.pop(0)()
        drain(gen0)
        ln_finalize(acc1[0], acc1[1], ("p1acc", 0), ("p1acc", 1))

        P.barrier()
        AR.reset(base_mark)

        maskM = AR.alloc([128, MASKW], BF16)
        P.op("pool", lambda h: h.dma_start(out=maskM, in_=mask_d), w=["maskM"], dma="maskld")
        kT = [AR.alloc([128, S], BF16) for _ in range(3)]
        vh = [AR.alloc([128, 16, 128], BF16) for _ in range(3)]
        qT = [AR.alloc([128, S], BF16) for _ in range(3)]
        NPT = 8
        LOOK = 6
        pt = [AR.alloc([128, 512], BF16) for _ in range(NPT)]
        rl = [AR.alloc([128, 512], F32) for _ in range(2)]
        ast = [AR.alloc([128, 512], BF16) for _ in range(2)]
        SB = [0, 1, 2]
        OB = [3, 4]
        LB = [5, 6]
        scale = float(HD) ** -0.5
        heads = [(seq, hh) for seq in range(2) for hh in range(NH)]

        def load_head(n):
            seq, hh = heads[n]
            i = n % 3
            tq = TQ[seq]
            P.op("sp", lambda h: h.dma_start(out=kT[i], in_=kT_s[seq][hh]), r=[("qk_s", seq)], w=[("kT", i)], dma=("kT", i))
            P.op("sp", lambda h: h.dma_start(out=vh[i], in_=v_s[seq][:, hh * 128:(hh + 1) * 128].rearrange("(kb p) d -> p kb d", p=128)),
                 r=[("v_s", seq)], w=[("vh", i)], dma=("vh", i))
            P.op("sp", lambda h: h.dma_start(out=qT[i][:, 0:tq], in_=qT_s[seq][hh]), r=[("qk_s", seq)], w=[("qT", i)], dma=("qT", i))

        steps = []
        qbc = 0
        for n, (seq, hh) in enumerate(heads):
            for qb in range(TQ[seq] // 512):
                q0 = qb * 512
                kbs = [kb for kb in range(16) if kb * 128 + 127 >= q0 - 1024 and kb * 128 <= q0 + 511 + 1024]
                oi = qbc % 2
                qbc += 1
                for kb in kbs:
                    steps.append(dict(n=n, i=n % 3, seq=seq, hh=hh, q0=q0, kb=kb, first=(kb == kbs[0]), last=(kb == kbs[-1]),
                                      oi=oi, head_first=(qb == 0 and kb == kbs[0])))
        ctr = {"s": 0, "pt": 0}

        def emit_qk(sp_):
            i, q0, kb = sp_["i"], sp_["q0"], sp_["kb"]
            sbk = SB[ctr["s"] % 3]
            ctr["s"] += 1
            pi = ctr["pt"] % NPT
            ctr["pt"] += 1
            mm(ps[sbk][:], kT[i][:, kb * 128:(kb + 1) * 128], qT[i][:, q0:q0 + 512], True, True,
               [("kT", i), ("qT", i)], [("ps", sbk)])
            P.op("act", lambda h: h.activation(out=pt[pi], in_=ps[sbk][:], func=AF.Exp, scale=scale, bias=negB),
                 r=[("ps", sbk), "negB"], w=[("pt", pi)])
            off = U0 - (kb * 128 - q0)
            P.op("dve", lambda h: h.tensor_tensor(out=pt[pi], in0=pt[pi], in1=maskM[:, off:off + 512], op=ALU.mult),
                 r=[("pt", pi), "maskM"], w=[("pt", pi)])
            return pi

        def emit_pv(sp_, pi):
            i, kb, oi = sp_["i"], sp_["kb"], sp_["oi"]
            ob, lbk = OB[oi], LB[oi]
            mm(ps[ob][:], vh[i][:, kb, :], pt[pi], sp_["first"], sp_["last"], [("vh", i), ("pt", pi)], [("ps", ob)])
            mm(ps[lbk][:], onesb, pt[pi], sp_["first"], sp_["last"], ["onesb", ("pt", pi)], [("ps", lbk)])
            if sp_["last"]:
                seq, hh, q0 = sp_["seq"], sp_["hh"], sp_["q0"]
                P.op("dve", lambda h: h.reciprocal(out=rl[oi], in_=ps[lbk][:]), r=[("ps", lbk)], w=[("rl", oi)])
                P.op("dve", lambda h: h.tensor_tensor(out=ast[oi], in0=ps[ob][:], in1=rl[oi], op=ALU.mult),
                     r=[("ps", ob), ("rl", oi)], w=[("ast", oi)])
                P.op("sp", lambda h: h.dma_start(out=attnT_s[seq][hh * 128:(hh + 1) * 128, q0:q0 + 512], in_=ast[oi]),
                     r=[("ast", oi)], w=[("attn_s", seq)], dma=("attn_st", oi))

        load_head(0)
        load_head(1)
        pend = []
        for sp_ in steps:
            if sp_["head_first"] and sp_["n"] >= 1 and sp_["n"] + 1 < len(heads):
                load_head(sp_["n"] + 1)
            pend.append((sp_, emit_qk(sp_)))
            if len(pend) > LOOK:
                a_, b_ = pend.pop(0)
                emit_pv(a_, b_)
        while pend:
            a_, b_ = pend.pop(0)
            emit_pv(a_, b_)

        P.barrier()
        AR.reset(base_mark)
        st["bank"] = 0

        R1 = AR.alloc([128, NFF, 512], BF16)
        r1_off = AR.off - NFF * 512 * 2
        hbuf = arena_t[:, r1_off // 2:(r1_off + 4 * D * 4) // 2].bitcast(F32).rearrange("p (a b) -> p a b", b=D)
        tail = r1_off + 4 * D * 4
        hf_tm = [arena_t[:, (tail + i * D * 2) // 2:(tail + (i + 1) * D * 2) // 2] for i in range(2)]
        assert tail + 2 * D * 2 <= r1_off + NFF * 512 * 2
        X0 = AR.alloc([128, 32, 512], BF16)
        inst = [AR.alloc([128, 512], F32) for _ in range(4)]
        ost = [AR.alloc([128, 512], F32) for _ in range(3)]
        uwin = [AR.alloc([128, 544], F32) for _ in range(4)]
        sgt = [AR.alloc([128, 512], F32) for _ in range(2)]
        acc = [AR.alloc([128, 512], F32) for _ in range(4)]
        ysq = [AR.alloc([128, 512], F32) for _ in range(2)]
        sqj = AR.alloc([128, 512], BF16)
        X0ALL = ["X0"] + [("X0c", c) for c in range(16)]
        ssp = AR.alloc([128, 4, 8], F32)
        rs4 = AR.alloc([128, 8], F32)

        print("phase3 arena", AR.off)
        c3 = {"in": 0, "o": 0, "uw": 0}

        def next_bank6():
            b = st["bank"]
            st["bank"] = (b + 1) % 8
            return b

        def normalize4(tidx, c0):
            cs = range(c0, c0 + 4)
            for c in cs:
                i4 = c % 4
                P.op("sp", lambda h, c=c, i4=i4: h.dma_start(out=acc[i4], in_=y_s[tidx, c * 128:(c + 1) * 128, :]), r=[("y_s", tidx)], w=[("p3acc", i4)], dma=("yld", i4))
            for c in cs:
                i4 = c % 4
                P.op("dve", lambda h, c=c, i4=i4: h.tensor_tensor(out=acc[i4], in0=acc[i4], in1=lnr, op=ALU.mult), r=[("p3acc", i4), "lnr"], w=[("p3acc", i4)])
            for c in cs:
                i4 = c % 4
                P.op("dve", lambda h, c=c, i4=i4: h.tensor_tensor(out=acc[i4], in0=acc[i4], in1=lnb, op=ALU.add), r=[("p3acc", i4), "lnb"], w=[("p3acc", i4)])
            for c in cs:
                i4 = c % 4
                P.op("act", lambda h, c=c, i4=i4: h.activation(out=X0[:, 16 + c, :], in_=acc[i4], func=AF.Silu, scale=lg[:, c:c + 1], bias=lb[:, c:c + 1]),
                     r=[("p3acc", i4), "lg", "lb"], w=[("X0c", c)])

        def attn_load(tidx):
            seq, t0 = tiles3[tidx]
            P.op("sp", lambda h: h.dma_start(out=X0[:, 0:16, :], in_=attnT_s[seq][:, t0:t0 + 512].rearrange("(c p) t -> p c t", p=128)),
                 r=[("attn_s", seq)], w=["X0"], dma="X0ld")

        def outproj(ti, seq, t0):
            def epi_for(cg):
                def epi(ts, b):
                    ii = c3["in"] % 4
                    c3["in"] += 1
                    r0 = t0 + ts * 128
                    P.op("sp", lambda h: h.dma_start(out=inst[ii], in_=xs_d[seq][r0:r0 + 128, cg * 512:(cg + 1) * 512]), w=[("inst", ii)], dma=("inst", ii))
                    P.op("dve", lambda h: h.tensor_tensor(out=hbuf[:, ts, cg * 512:(cg + 1) * 512], in0=ps[b][:], in1=inst[ii], op=ALU.add),
                         r=[("ps", b), ("inst", ii)], w=["R1"])
                    P.op("act", lambda h: h.activation(out=sqj, in_=hbuf[:, ts, cg * 512:(cg + 1) * 512], func=AF.Square, accum_out=ssp[:, ts, cg:cg + 1]),
                         r=["R1"], w=["sqj", "ssp"])
                return epi
            P.op("dve", lambda h: h.memset(ssp, 0.0), w=["ssp"])
            for cg in range(8):
                wb = [(w_out_v[:, kb * 8:(kb + 1) * 8, cg * 512:(cg + 1) * 512], 8) for kb in range(4)]
                tm_group6(wb, lambda kg, ts: X0[:, kg, ts * 128:(ts + 1) * 128], X0ALL, epi_for(cg))
            P.op("dve", lambda h: h.tensor_reduce(out=rs4[:, 0:4], in_=ssp, axis=AX.X, op=ALU.add), r=["ssp"], w=["rs4a"])
            P.op("act", lambda h: h.activation(out=rs4[:, 4:8], in_=rs4[:, 0:4], func=AF.Sqrt, scale=1.0 / D, bias=epsb), r=["rs4a", "epsb"], w=["rs4b"])
            P.op("dve", lambda h: h.reciprocal(out=rs4[:, 0:4], in_=rs4[:, 4:8]), r=["rs4b"], w=["rs4a"])
            for ts in range(4):
                fi = ts % 2
                P.op("sp", lambda h, ts=ts: h.dma_start(out=h_s[ti, ts * 128:(ts + 1) * 128, :], in_=hbuf[:, ts, :]), r=["R1"], w=["h_s"], dma="hs_st")
                P.op("act", lambda h, ts=ts, fi=fi: h.activation(out=hf_tm[fi], in_=hbuf[:, ts, :], func=AF.Copy, scale=rs4[:, ts:ts + 1]),
                     r=["R1", "rs4a"], w=[("hf_tm", fi)])
                transposes_to6(lambda kc, fi=fi: hf_tm[fi][:, kc * 128:(kc + 1) * 128], [("hf_tm", fi)], X0, X0ALL, ts, gffn)

        def tm_group6(wblocks, lhs, lhs_keys, epi, hook=None):
            banks = [next_bank6() for _ in range(4)]
            nkt = sum(nk for _, nk in wblocks)
            kg0 = 0
            for dap, nk in wblocks:
                view, wkey = wload(dap, nk, 512)
                for ts in range(4):
                    for k in range(nk):
                        kg = kg0 + k
                        mm(ps[banks[ts]][:], lhs(kg, ts), view[:, k, :], kg == 0, kg == nkt - 1,
                           [wkey] + lhs_keys, [("ps", banks[ts])])
                kg0 += nk
            for ts in range(4):
                epi(ts, banks[ts])
            if hook is not None:
                hook()

        def transposes_to6(src_fn, src_keys, dst, dst_key, ts, gcol):
            for g in range(4):
                b = next_bank6()
                for k in range(8):
                    kc = g * 8 + k
                    P.op("pe", lambda h, b=b, k=k, kc=kc: h.transpose(psb[b][:, k * 128:(k + 1) * 128], src_fn(kc), ident),
                         r=src_keys + ["ident"], w=[("ps", b)])
                P.op("dve", lambda h, b=b, g=g: h.tensor_tensor(
                    out=dst[:, g * 8:(g + 1) * 8, ts * 128:(ts + 1) * 128],
                    in0=psb[b].rearrange("p (k t) -> p k t", t=128),
                    in1=bc(gcol[:, g * 8:(g + 1) * 8].unsqueeze(2), [128, 8, 128]), op=ALU.mult),
                    r=[("ps", b), "gmix", "gffn"], w=list(dst_key))

        def gateup(gen=None):
            for gi in range(NFF // 2):
                if gi >= 1:
                    drive(gen, 8)
                c0 = gi * 256
                wA = [w_gate_v[:, kb * 16:(kb + 1) * 16, c0:c0 + 256] for kb in range(2)]
                wB = [w_up_v[:, kb * 16:(kb + 1) * 16, c0:c0 + 256] for kb in range(2)]
                bA = [next_bank6(), next_bank6()]
                bB = [next_bank6(), next_bank6()]
                for kb2 in range(2):
                    for blocks, banks in ((wA, bA), (wB, bB)):
                        view, wkey = wload(blocks[kb2], 16, 256)
                        for j in range(2):
                            for k in range(16):
                                kg = kb2 * 16 + k
                                mm(ps[banks[j]][:], view[:, k, j * 128:(j + 1) * 128], X0[:, kg, :], kg == 0, kg == 31,
                                   [wkey] + X0ALL, [("ps", banks[j])])
                for j in range(2):
                    i2 = uid() % 2
                    ffc = gi * 2 + j
                    P.op("act", lambda h, b=bA[j], i2=i2: h.activation(out=sgt[i2], in_=ps[b][:], func=AF.Silu), r=[("ps", bA[j])], w=[("sgt", i2)])
                    P.op("dve", lambda h, b=bB[j], i2=i2, ffc=ffc: h.tensor_tensor(out=R1[:, ffc, :], in0=ps[b][:], in1=sgt[i2], op=ALU.mult),
                         r=[("ps", bB[j]), ("sgt", i2)], w=["R1"])

        def down(ti, seq, t0, hooks):
            for cg in range(8):
                wb = []
                for fb in range(11):
                    nk = 8 if fb < 10 else 6
                    wb.append((w_down_v[:, fb * 8:fb * 8 + nk, cg * 512:(cg + 1) * 512], nk))

                def epi(ts, b, cg=cg):
                    ii = c3["in"] % 4
                    c3["in"] += 1
                    oi = c3["o"] % 3
                    c3["o"] += 1
                    r0 = t0 + ts * 128
                    P.op("sp", lambda h: h.dma_start(out=inst[ii], in_=h_s[ti, ts * 128:(ts + 1) * 128, cg * 512:(cg + 1) * 512]), r=["h_s"], w=[("inst", ii)], dma=("inst", ii))
                    P.op("dve", lambda h: h.tensor_tensor(out=ost[oi], in0=ps[b][:], in1=inst[ii], op=ALU.add), r=[("ps", b), ("inst", ii)], w=[("ost", oi)])
                    P.op("sp", lambda h: h.dma_start(out=ys_d[seq][r0:r0 + 128, cg * 512:(cg + 1) * 512], in_=ost[oi]), r=[("ost", oi)], dma=("out", oi))
                tm_group6(wb, lambda kg, ts: R1[:, kg, ts * 128:(ts + 1) * 128], ["R1"], epi, hook=(hooks[cg] if hooks else None))

        attn_load(0)
        for c0 in (0, 4, 8, 12):
            normalize4(0, c0)
        for ti, (seq, t0) in enumerate(tiles3):
            outproj(ti, seq, t0)
            hooks = None
            gen = None
            if ti + 1 < len(tiles3):
                g3 = conv_gen(ti + 1, acc, uwin, ysq, next_bank6, "p3", immediate=True)
                nt = ti + 1

                def fz():
                    ln_finalize(acc[0], acc[1], ("p3acc", 0), ("p3acc", 1))
                hooks = [(lambda g3=g3, nt=nt: (attn_load(nt), run_pairs(g3, 1))),
                         (lambda g3=g3: run_pairs(g3, 2)),
                         (lambda g3=g3: run_pairs(g3, 1)),
                         (lambda g3=g3: run_pairs(g3, 2)),
                         (lambda g3=g3: run_pairs(g3, 1)),
                         (lambda g3=g3: (run_pairs(g3, 1), drain(g3), fz())),
                         (lambda nt=nt: (normalize4(nt, 0), normalize4(nt, 4))),
                         (lambda nt=nt: (normalize4(nt, 8), normalize4(nt, 12)))]
            gateup(gen)
            down(ti, seq, t0, hooks)

        fin = [("out", 0), ("out", 1), ("out", 2)]
        if DEBUG:
            fin = list(P.dma_cnt.keys())
        P.emit(nc, es, final_wait_keys=fin)
    return nc


def _mask_table():
    i = np.arange(128)[:, None]
    u = np.arange(MASKW)[None, :]
    dlt = np.abs(i - u + U0)
    c = (dlt <= 64).astype(np.float32)
    c += ((dlt % 4 == 0) & (dlt <= 256)).astype(np.float32)
    c += ((dlt % 16 == 0) & (dlt <= 1024)).astype(np.float32)
    return np.ascontiguousarray(c, dtype=np.float32)


def _rope_tables(rev):
    half = HD // 2
    freqs = (10000.0 ** (-np.arange(half, dtype=np.float32) * 2.0 / HD)).astype(np.float32)
    pos = np.arange(S, dtype=np.float32)
    if rev:
        pos = pos[::-1]
    ang = (pos[:, None] * freqs[None, :]).astype(np.float32)
    cos = np.cos(ang).astype(np.float32).reshape(16, 128, 64).transpose(1, 0, 2)
    sin = np.sin(ang).astype(np.float32).reshape(16, 128, 64).transpose(1, 0, 2)
    return np.ascontiguousarray(cos), np.ascontiguousarray(sin)


def _col(v, n):
    return np.ascontiguousarray(np.asarray(v, np.float32).reshape(n, 128).T)


_NC_CACHE = {}


def make_in_maps(x_prompt, x_sample, norm_mix_g, w_in, q_norm_g, k_norm_g, conv_w, conv_b, conv_ln_g, conv_ln_b,
                 w_out, norm_ffn_g, w_gate, w_up, w_down):
    f = lambda a: np.ascontiguousarray(np.asarray(a, dtype=np.float32))
    x_prompt = f(x_prompt); x_sample = f(x_sample)
    shared = dict(
        w_in=f(w_in[0]), w_out=f(w_out[0]), w_gate=f(w_gate[0]), w_up=f(w_up[0]), w_down=f(w_down[0]),
        gmix=_col(norm_mix_g[0], 32), gffn=_col(norm_ffn_g[0], 32),
        gq=np.ascontiguousarray(np.broadcast_to(f(q_norm_g[0])[None, :], (128, 128))),
        gk=np.ascontiguousarray(np.broadcast_to(f(k_norm_g[0])[None, :], (128, 128))),
        cb=_col(conv_b[0], 16), lg=_col(conv_ln_g[0], 16), lb=_col(conv_ln_b[0], 16),
        maskm=_mask_table(),
    )
    cwn = f(conv_w[0])
    cw_f = np.ascontiguousarray(cwn.reshape(TAPS, 16, 128).transpose(2, 1, 0))
    cw_r = np.ascontiguousarray(cwn[::-1].reshape(TAPS, 16, 128).transpose(2, 1, 0))
    cos_f, sin_f = _rope_tables(False)
    cos_r, sin_r = _rope_tables(True)
    in_maps = []
    for c in range(8):
        odd = c % 2 == 1
        xb = x_sample[c // 2]
        if odd:
            xb = np.ascontiguousarray(xb[::-1])
        m = dict(shared)
        m.update(xa=x_prompt[c], xb=xb, cwa=cw_f, cwb=cw_r if odd else cw_f,
                 cosa=cos_f, sina=sin_f, cosb=cos_r if odd else cos_f, sinb=sin_r if odd else sin_f)
        in_maps.append(m)
    return in_maps


def kernel(**inputs):
    in_maps = make_in_maps(**inputs)
    if "nc" not in _NC_CACHE:
        _NC_CACHE["nc"] = build_program()
    nc = _NC_CACHE["nc"]
    res = run_bass_kernel_spmd(nc, in_maps, core_ids=list(range(8)))
    y_prompt = np.empty((8, S, D), np.float32)
    y_sample = np.empty((4, S, D), np.float32)
    for c in range(8):
        r = res.results[c]
        y_prompt[c] = r["ya"]
        yb = np.asarray(r["yb"], np.float32)
        if c % 2 == 0:
            y_sample[c // 2, 0:1024] = yb
        else:
            y_sample[c // 2, 1024:2048] = yb[::-1]
    if DEBUG:
        kernel.last = res
    return (y_prompt, y_sample)
```

```python
import contextlib
import numpy as np
import concourse.bass as bass
import concourse.mybir as mybir
from concourse.bass_utils import run_bass_kernel_spmd

F32 = mybir.dt.float32
BF16 = mybir.dt.bfloat16
ALU = mybir.AluOpType
AF = mybir.ActivationFunctionType
AX = mybir.AxisListType

D = 4096
S = 2048
NH = 16
HD = 128
AW = 2048
CW = 2048
TAPS = 31
DFF = 11008
NFF = DFF // 128
INC = 10240
EPS = 1e-6
NSLOT = 4
U0 = 1536
MASKW = 3968
DEBUG = False

ENGS = ("pe", "act", "dve", "pool", "sp")
SEM_CHUNK = 30000


class Ins:
    __slots__ = ("eng", "fn", "deps", "dmadeps", "sig", "cnt", "dma_key", "idx")


class Prog:
    def __init__(self):
        self.ins = []
        self.lastw = {}
        self.readers = {}
        self.dma_cnt = {}
        self.per_eng = {e: [] for e in ENGS}
        self.pending = {}

    def op(self, eng, fn, r=(), w=(), dma=None):
        i = Ins()
        i.eng = eng
        i.fn = fn
        i.dma_key = dma
        i.sig = False
        i.cnt = None
        deps = {}
        dmadeps = {}

        def add(d, raw):
            if d.dma_key is not None:
                dmadeps[d.dma_key] = self.dma_cnt[d.dma_key]
                return
            if d.eng == eng:
                if eng == "pe" or not raw:
                    return
            cur = deps.get(d.eng)
            if cur is None or d.idx > cur.idx:
                deps[d.eng] = d

        for k in r:
            lw = self.lastw.get(k)
            if lw is not None:
                add(lw, True)
        for k in w:
            lw = self.lastw.get(k)
            if lw is not None:
                add(lw, False)
            for rd in self.readers.get(k, {}).values():
                add(rd, False)
        pend = self.pending.pop(eng, None)
        if pend is not None:
            for d in pend[0]:
                if d.eng != eng:
                    cur = deps.get(d.eng)
                    if cur is None or d.idx > cur.idx:
                        deps[d.eng] = d
            for k, v in pend[1].items():
                dmadeps[k] = max(dmadeps.get(k, 0), v)
        i.deps = list(deps.values())
        i.dmadeps = dmadeps
        i.idx = len(self.ins)
        self.ins.append(i)
        self.per_eng[eng].append(i)
        if dma is not None:
            self.dma_cnt[dma] = self.dma_cnt.get(dma, 0) + 1
        rk = ("dma", dma) if dma is not None else eng
        for k in r:
            self.readers.setdefault(k, {})[rk] = i
        for k in w:
            self.lastw[k] = i
            self.readers[k] = {}
        return i

    def barrier(self):
        lasts = []
        for e in ENGS:
            for i in reversed(self.per_eng[e]):
                if i.dma_key is None:
                    lasts.append(i)
                    break
        snap = dict(self.dma_cnt)
        for e in ENGS:
            self.pending[e] = (list(lasts), dict(snap))

    def emit(self, nc, es, final_wait_keys=()):
        for i in self.ins:
            for d in i.deps:
                d.sig = True
        nsig = {}
        for e in ENGS:
            c = 0
            for i in self.per_eng[e]:
                if i.sig:
                    c += 1
                    i.cnt = c
            nsig[e] = c
        esem = {}
        for e in ENGS:
            n = max(1, -(-nsig[e] // SEM_CHUNK))
            esem[e] = [es.enter_context(nc.semaphore(f"s_{e}_{j}")) for j in range(n)]
        dsem = {}
        for n_, k in enumerate(self.dma_cnt):
            dsem[k] = es.enter_context(nc.semaphore(f"d_{n_}"))
        block = es.enter_context(nc.Block())
        prog = self

        def run_engine(e, h):
            waited = {}
            for i in prog.per_eng[e]:
                for d in i.deps:
                    c = d.cnt - 1
                    ch = c // SEM_CHUNK
                    v = c % SEM_CHUNK + 1
                    done = False
                    for (de, ch2), wv in waited.items():
                        if de == d.eng and (ch2 > ch or (ch2 == ch and wv >= v)):
                            done = True
                            break
                    if not done:
                        h.wait_ge(esem[d.eng][ch], v)
                        waited[(d.eng, ch)] = v
                for k, cntv in i.dmadeps.items():
                    kk = ("dma", k)
                    v = 16 * cntv
                    if waited.get(kk, 0) < v:
                        h.wait_ge(dsem[k], v)
                        waited[kk] = v
                bi = i.fn(h)
                if i.dma_key is not None:
                    bi.then_inc(dsem[i.dma_key], 16)
                elif i.sig:
                    c = i.cnt - 1
                    bi.then_inc(esem[e][c // SEM_CHUNK], 1)
            if e == "sp":
                for k in final_wait_keys:
                    h.wait_ge(dsem[k], 16 * prog.dma_cnt[k])

        @block.tensor
        def _(h):
            run_engine("pe", h)

        @block.scalar
        def _(h):
            run_engine("act", h)

        @block.vector
        def _(h):
            run_engine("dve", h)

        @block.gpsimd
        def _(h):
            run_engine("pool", h)

        @block.sync
        def _(h):
            run_engine("sp", h)


class Arena:
    def __init__(self, tensor, nbytes):
        self.t = tensor
        self.n = nbytes
        self.off = 0

    def alloc(self, shape, dt):
        esz = 4 if dt == F32 else 2
        free = 1
        for s_ in shape[1:]:
            free *= s_
        nb = free * esz
        self.off = (self.off + 31) // 32 * 32
        assert self.off + nb <= self.n, ("SBUF arena overflow", self.off, nb, self.n)
        a = self.t[:, self.off // 2:(self.off + nb) // 2]
        self.off += nb
        if dt == F32:
            a = a.bitcast(F32)
        if len(shape) == 3:
            a = a.rearrange("p (a b) -> p a b", b=shape[2])
        return a

    def mark(self):
        return self.off

    def reset(self, m):
        self.off = m


def bc(ap, shape):
    return ap.broadcast_to(shape)


def build_program():
    nc = bass.Bass("TRN2", target_bir_lowering=False)
    dt_in = lambda name, shape: nc.dram_tensor(name, shape, F32, kind="ExternalInput").ap()
    xs_d = [dt_in("xa", [S, D]), dt_in("xb", [S, D])]
    w_in = dt_in("w_in", [D, INC])
    w_out = dt_in("w_out", [D, D])
    w_gate = dt_in("w_gate", [D, DFF])
    w_up = dt_in("w_up", [D, DFF])
    w_down = dt_in("w_down", [DFF, D])
    gmix_d = dt_in("gmix", [128, 32])
    gffn_d = dt_in("gffn", [128, 32])
    gq_d = dt_in("gq", [128, 128])
    gk_d = dt_in("gk", [128, 128])
    cw_d = [dt_in("cwa", [128, 16, TAPS]), dt_in("cwb", [128, 16, TAPS])]
    cb_d = dt_in("cb", [128, 16])
    lg_d = dt_in("lg", [128, 16])
    lb_d = dt_in("lb", [128, 16])
    cos_d = [dt_in("cosa", [128, 16, 64]), dt_in("cosb", [128, 16, 64])]
    sin_d = [dt_in("sina", [128, 16, 64]), dt_in("sinb", [128, 16, 64])]
    mask_d = dt_in("maskm", [128, MASKW])
    ys_d = [nc.dram_tensor("ya", [S, D], F32, kind="ExternalOutput").ap(),
            nc.dram_tensor("yb", [1024, D], F32, kind="ExternalOutput").ap()]
    skind = "ExternalOutput" if DEBUG else "Internal"
    TQ = [2048, 1024]
    UW = [15 + 2048 + 15, 15 + 1152]
    qT_s = [nc.dram_tensor(f"qT_s{s}", [NH, 128, TQ[s]], BF16, kind=skind).ap() for s in range(2)]
    kT_s = [nc.dram_tensor(f"kT_s{s}", [NH, 128, S], BF16, kind=skind).ap() for s in range(2)]
    v_s = [nc.dram_tensor(f"v_s{s}", [S, AW], BF16, kind=skind).ap() for s in range(2)]
    uT_s = [nc.dram_tensor(f"uT_s{s}", [CW, UW[s]], F32, kind=skind).ap() for s in range(2)]
    attnT_s = [nc.dram_tensor(f"attnT_s{s}", [AW, TQ[s]], BF16, kind=skind).ap() for s in range(2)]
    h_s = nc.dram_tensor("h_s", [6, 512, D], F32, kind=skind).ap()
    y_s = nc.dram_tensor("y_s", [6, CW, 512], F32, kind=skind).ap()

    w_in_v = w_in.rearrange("(kc p) n -> p kc n", p=128)
    w_out_v = w_out.rearrange("(kc p) n -> p kc n", p=128)
    w_gate_v = w_gate.rearrange("(kc p) n -> p kc n", p=128)
    w_up_v = w_up.rearrange("(kc p) n -> p kc n", p=128)
    w_down_v = w_down.rearrange("(kc p) n -> p kc n", p=128)

    P = Prog()
    es = contextlib.ExitStack()
    with es:
        ARENA_BYTES = 207 * 1024
        arena_t = es.enter_context(nc.sbuf_tensor("arena", [128, ARENA_BYTES // 2], BF16))
        AR = Arena(arena_t, ARENA_BYTES)
        ps = [es.enter_context(nc.psum_tensor(f"ps{i}", [128, 512], F32)) for i in range(8)]
        psb = [p_[:].bitcast(BF16) for p_ in ps]
        st = {"bank": 0, "slot": 0, "n": 0}

        def next_bank():
            b = st["bank"]
            st["bank"] = (b + 1) % 8
            return b

        def uid():
            st["n"] += 1
            return st["n"]

        ident = AR.alloc([128, 128], BF16)
        identf = AR.alloc([128, 128], F32)
        onesf = AR.alloc([128, 128], F32)
        onesb = AR.alloc([128, 128], BF16)
        gmix = AR.alloc([128, 32], F32)
        gffn = AR.alloc([128, 32], F32)
        gq = AR.alloc([128, 128], F32)
        gk = AR.alloc([128, 128], F32)
        cw = [AR.alloc([128, 16, TAPS], F32) for _ in range(2)]
        cb = AR.alloc([128, 16], F32)
        lg = AR.alloc([128, 16], F32)
        lb = AR.alloc([128, 16], F32)
        epsb = AR.alloc([128, 1], F32)
        negB = AR.alloc([128, 1], F32)
        sm = AR.alloc([128, 8], F32)
        wslots = [AR.alloc([128, 4096], BF16) for _ in range(NSLOT)]
        s1acc = AR.alloc([128, 512], F32)
        s2acc = AR.alloc([128, 512], F32)
        lnr = AR.alloc([128, 512], F32)
        lnb = AR.alloc([128, 512], F32)
        tiles3 = [(0, t0) for t0 in (0, 512, 1024, 1536)] + [(1, 0), (1, 512)]
        y_keys = {}

        def conv_gen(tidx, acc, uwin, ysq, bankfn, pfx, immediate=False):
            seq, t0 = tiles3[tidx]
            cwt = cw[seq]
            cwk = "cw%d" % seq
            nu = len(uwin) // 2
            prev_stats = None
            for p in range(8):
                chains = []
                for n_, c in enumerate((2 * p, 2 * p + 1)):
                    ui = (p % nu) * 2 + n_
                    i_ = P.op("sp", lambda h, ui=ui, c=c: h.dma_start(out=uwin[ui][:, 0:542], in_=uT_s[seq][c * 128:(c + 1) * 128, t0:t0 + 542]),
                              r=[("uT", seq)], w=[(pfx + "uwin", ui)], dma=(pfx + "uwin", ui))
                    for k_ in list(P.dma_cnt):
                        if k_ == "uT_st" or (isinstance(k_, tuple) and k_[0] == "uT_st"):
                            i_.dmadeps[k_] = P.dma_cnt[k_]
                    chains.append((c, ui, 2 * n_))
                if prev_stats is not None:
                    prev_stats()
                    prev_stats = None
                for (c, ui, a0) in chains:
                    P.op("dve", lambda h, c=c, ui=ui, a0=a0: h.tensor_scalar(out=acc[a0], in0=uwin[ui][:, 0:512], scalar1=cwt[:, c, 0:1], scalar2=cb[:, c:c + 1], op0=ALU.mult, op1=ALU.add),
                         r=[(pfx + "uwin", ui), cwk, "cb"], w=[(pfx + "acc", a0)])
                yield None
                cur = 0
                for tap in range(1, TAPS):
                    nxt = 1 - cur
                    for (c, ui, a0) in chains:
                        P.op("dve", lambda h, tap=tap, cur=cur, nxt=nxt, c=c, ui=ui, a0=a0: h.scalar_tensor_tensor(
                            out=acc[a0 + nxt], in0=uwin[ui][:, tap:tap + 512], scalar=cwt[:, c, tap:tap + 1], in1=acc[a0 + cur], op0=ALU.mult, op1=ALU.add),
                            r=[(pfx + "uwin", ui), cwk, (pfx + "acc", a0 + cur)], w=[(pfx + "acc", a0 + nxt)])
                    cur = nxt
                    if tap < TAPS - 1:
                        yield None
                fins = []
                for n_, (c, ui, a0) in enumerate(chains):
                    fin = a0 + cur
                    fins.append((n_, c, fin))
                    P.op("act", lambda h, fin=fin, n_=n_: h.activation(out=ysq[n_], in_=acc[fin], func=AF.Square), r=[(pfx + "acc", fin)], w=[(pfx + "ysq", n_)])
                    P.op("sp", lambda h, fin=fin, c=c: h.dma_start(out=y_s[tidx, c * 128:(c + 1) * 128, :], in_=acc[fin]), r=[(pfx + "acc", fin)], w=[("y_s", tidx)], dma="yst")

                def stats(fins=fins):
                    for (n_, c, fin) in fins:
                        for (src, skey, dst, dkey) in ((acc[fin], (pfx + "acc", fin), s1acc, "s1acc"), (ysq[n_], (pfx + "ysq", n_), s2acc, "s2acc")):
                            b = bankfn()
                            mm(ps[b][:], onesf, src, True, True, ["onesf", skey], [("ps", b)])
                            if c == 0:
                                P.op("dve", lambda h, b=b, dst=dst: h.tensor_copy(out=dst, in_=ps[b][:]), r=[("ps", b)], w=[dkey])
                            else:
                                P.op("dve", lambda h, b=b, dst=dst: h.tensor_tensor(out=dst, in0=ps[b][:], in1=dst, op=ALU.add), r=[("ps", b), dkey], w=[dkey])
                if immediate:
                    stats()
                    yield "P"
                else:
                    prev_stats = stats
                    yield "B"
            if prev_stats is not None:
                prev_stats()
            yield "B"

        def drive(gen, n=8):
            if gen is None:
                return
            for _ in range(n):
                try:
                    if next(gen) == "B":
                        break
                except StopIteration:
                    break

        def run_pairs(gen, n):
            seen = 0
            while seen < n:
                try:
                    if next(gen) == "P":
                        seen += 1
                except StopIteration:
                    break

        def drain(gen):
            if gen is None:
                return
            for _ in gen:
                pass

        def ln_finalize(tmp0, tmp1, k0, k1):
            P.op("dve", lambda h: h.tensor_single_scalar(out=s1acc, in_=s1acc, scalar=1.0 / CW, op=ALU.mult), r=["s1acc"], w=["s1acc"])
            P.op("dve", lambda h: h.tensor_tensor(out=tmp0, in0=s1acc, in1=s1acc, op=ALU.mult), r=["s1acc"], w=[k0])
            P.op("dve", lambda h: h.scalar_tensor_tensor(out=tmp1, in0=s2acc, scalar=1.0 / CW, in1=tmp0, op0=ALU.mult, op1=ALU.subtract),
                 r=["s2acc", k0], w=[k1])
            P.op("act", lambda h: h.activation(out=tmp0, in_=tmp1, func=AF.Sqrt, bias=epsb), r=[k1, "epsb"], w=[k0])
            P.op("dve", lambda h: h.reciprocal(out=lnr, in_=tmp0), r=[k0], w=["lnr"])
            P.op("dve", lambda h: h.scalar_tensor_tensor(out=lnb, in0=s1acc, scalar=-1.0, in1=lnr, op0=ALU.mult, op1=ALU.mult), r=["s1acc", "lnr"], w=["lnb"])

        def ld(dst, src, key):
            P.op("sp", lambda h: h.dma_start(out=dst, in_=src), w=[key], dma="const")

        ld(gmix, gmix_d, "gmix"); ld(gffn, gffn_d, "gffn"); ld(gq, gq_d, "gq"); ld(gk, gk_d, "gk")
        ld(cw[0], cw_d[0], "cw0"); ld(cw[1], cw_d[1], "cw1"); ld(cb, cb_d, "cb"); ld(lg, lg_d, "lg"); ld(lb, lb_d, "lb")
        P.op("dve", lambda h: h.memset(identf, 0.0), w=["identf"])
        P.op("dve", lambda h: h.memset(onesf, 1.0), w=["onesf"])
        P.op("dve", lambda h: h.memset(epsb, EPS), w=["epsb"])
        P.op("pool", lambda h: h.affine_select(out=identf, in_=onesf, pattern=[[-1, 128]], compare_op=ALU.is_equal,
                                               fill=0.0, base=0, channel_multiplier=1), r=["onesf"], w=["identf"])
        P.op("dve", lambda h: h.tensor_copy(out=ident, in_=identf), r=["identf"], w=["ident"])
        P.op("dve", lambda h: h.tensor_copy(out=onesb, in_=onesf), r=["onesf"], w=["onesb"])
        P.op("dve", lambda h: h.tensor_reduce(out=sm[:, 0:1], in_=gq, axis=AX.X, op=ALU.max, apply_absolute_value=True), r=["gq"], w=["sm0"])
        P.op("dve", lambda h: h.tensor_reduce(out=sm[:, 1:2], in_=gk, axis=AX.X, op=ALU.max, apply_absolute_value=True), r=["gk"], w=["sm1"])
        P.op("dve", lambda h: h.tensor_tensor(out=sm[:, 2:3], in0=sm[:, 0:1], in1=sm[:, 1:2], op=ALU.mult), r=["sm0", "sm1"], w=["sm2"])
        P.op("dve", lambda h: h.tensor_single_scalar(out=negB, in_=sm[:, 2:3], scalar=-(128.0 ** 0.5), op=ALU.mult), r=["sm2"], w=["negB"])
        zt = AR.alloc([128, 16, 15], F32)
        P.op("dve", lambda h: h.memset(zt, 0.0), w=["zt"])
        for s in range(2):
            P.op("sp", lambda h, s=s: h.dma_start(out=uT_s[s][:, 0:15].rearrange("(c p) z -> p c z", p=128), in_=zt), r=["zt"], w=[("uT", s)], dma="uT_st")
        P.op("sp", lambda h: h.dma_start(out=uT_s[0][:, 15 + 2048:15 + 2048 + 15].rearrange("(c p) z -> p c z", p=128), in_=zt), r=["zt"], w=[("uT", 0)], dma="uT_st")

        deferred = []

        def wload(dram_ap, nk, ncols):
            i = st["slot"] % NSLOT
            st["slot"] += 1
            view = wslots[i][:, 0:nk * ncols].rearrange("p (k n) -> p k n", n=ncols)
            P.op("pool", lambda h: h.dma_start(out=view, in_=dram_ap), w=[("w", i)], dma=("w", i))
            return view, ("w", i)

        def mm(out, lhsT, rhs, start, stop, r, w):
            P.op("pe", lambda h: h.matmul(out, lhsT=lhsT, rhs=rhs, start=start, stop=stop), r=r, w=w)

        def tm_group(wblocks, lhs, lhs_keys, epi, nts=4, epi_all=None):
            banks = [next_bank() for _ in range(nts)]
            nkt = sum(nk for _, nk in wblocks)
            kg0 = 0
            for dap, nk in wblocks:
                view, wkey = wload(dap, nk, 512)
                for ts in range(nts):
                    for k in range(nk):
                        kg = kg0 + k
                        mm(ps[banks[ts]][:], lhs(kg, ts), view[:, k, :], kg == 0, kg == nkt - 1,
                           [wkey] + lhs_keys, [("ps", banks[ts])])
                kg0 += nk
            while deferred:
                deferred.pop(0)()
            if epi_all is not None:
                epi_all(banks)
            else:
                for ts in range(nts):
                    epi(ts, banks[ts])

        def fm_pair(wA, wB, rhs, rhs_keys, NT, epi):
            bA = [next_bank(), next_bank()]
            bB = [next_bank(), next_bank()]
            for kb2 in range(2):
                for blocks, banks in ((wA, bA), (wB, bB)):
                    view, wkey = wload(blocks[kb2], 16, 256)
                    for j in range(2):
                        for k in range(16):
                            kg = kb2 * 16 + k
                            mm(ps[banks[j]][:, 0:NT], view[:, k, j * 128:(j + 1) * 128], rhs(kg), kg == 0, kg == 31,
                               [wkey] + rhs_keys, [("ps", banks[j])])
            while deferred:
                deferred.pop(0)()
            for j in range(2):
                epi(j, bA[j], bB[j])

        def transposes_to(src_fn, src_keys, dst, dst_key, ts, gcol):
            for g in range(4):
                b = next_bank()
                for k in range(8):
                    kc = g * 8 + k
                    P.op("pe", lambda h, b=b, k=k, kc=kc: h.transpose(psb[b][:, k * 128:(k + 1) * 128], src_fn(kc), ident),
                         r=src_keys + ["ident"], w=[("ps", b)])
                P.op("dve", lambda h, b=b, g=g: h.tensor_tensor(
                    out=dst[:, g * 8:(g + 1) * 8, ts * 128:(ts + 1) * 128],
                    in0=psb[b].rearrange("p (k t) -> p k t", t=128),
                    in1=bc(gcol[:, g * 8:(g + 1) * 8].unsqueeze(2), [128, 8, 128]), op=ALU.mult),
                    r=[("ps", b), "gmix", "gffn"], w=[dst_key])

        base_mark = AR.mark()

        hnT = [AR.alloc([128, 32, 512], BF16) for _ in range(2)]
        xst = [AR.alloc([128, D], F32)] * 2
        hn_tm = [AR.alloc([128, D], BF16)] * 2
        cos_t = AR.alloc([128, 4, 64], F32)
        sin_t = AR.alloc([128, 4, 64], F32)
        rtab = [[AR.alloc([128, 4, 64], F32) for _ in range(4)] for _ in range(2)]
        sqt = [AR.alloc([128, 512], F32) for _ in range(4)]
        qn = [AR.alloc([128, 4, 128], F32) for _ in range(4)]
        rt = [AR.alloc([128, 4, 64], F32) for _ in range(8)]
        qo = [AR.alloc([128, 4, 128], BF16) for _ in range(4)]
        ssq = [AR.alloc([128, 8], F32) for _ in range(4)]
        qstage = [AR.alloc([128, 4, 512], BF16)] * 2
        vst = [AR.alloc([128, 512], BF16) for _ in range(2)]
        sg = [AR.alloc([128, 512], F32)] * 2
        ust = [AR.alloc([128, 512], F32) for _ in range(2)]
        acc1 = [AR.alloc([128, 512], F32) for _ in range(4)]
        uwin1 = [AR.alloc([128, 544], F32) for _ in range(2)]
        ysq1 = [AR.alloc([128, 512], F32) for _ in range(2)]
        print("phase1 arena", AR.off)
        rst1 = [AR.alloc([128, 2], F32) for _ in range(2)]
        tiles1 = [dict(seq=0, t0=t0, q=True, conv=512) for t0 in (0, 512, 1024, 1536)]
        tiles1 += [dict(seq=1, t0=0, q=True, conv=512), dict(seq=1, t0=512, q=True, conv=512),
                   dict(seq=1, t0=1024, q=False, conv=128), dict(seq=1, t0=1536, q=False, conv=0)]
        cnt1 = {"x": 0, "u": 0, "v": 0, "qk": 0}

        def prep_tables(ti):
            tl = tiles1[ti]
            blk0 = tl["t0"] // 128
            sq_ = tl["seq"]
            P.op("sp", lambda h: h.dma_start(out=cos_t, in_=cos_d[sq_][:, blk0:blk0 + 4, :]), w=["cos_t"], dma="rope_ld")
            P.op("sp", lambda h: h.dma_start(out=sin_t, in_=sin_d[sq_][:, blk0:blk0 + 4, :]), w=["sin_t"], dma="rope_ld")
            for qk_i, (gv, gkey) in enumerate(((gq, "gq"), (gk, "gk"))):
                g1 = bc(gv[:, 0:64].unsqueeze(1), [128, 4, 64])
                g2 = bc(gv[:, 64:128].unsqueeze(1), [128, 4, 64])
                for n_, (tab, gg) in enumerate(((cos_t, g1), (sin_t, g2), (sin_t, g1), (cos_t, g2))):
                    P.op("dve", lambda h, tab=tab, gg=gg, dst=rtab[qk_i][n_]: h.tensor_tensor(out=dst, in0=tab, in1=gg, op=ALU.mult),
                         r=["cos_t", "sin_t", gkey], w=[("rtab", qk_i)])

        def prep_norm(ti, ts):
            tl = tiles1[ti]
            xb_ = cnt1["x"] % 2
            cnt1["x"] += 1
            r0 = tl["t0"] + ts * 128
            P.op("sp", lambda h, xb_=xb_, r0=r0, s=tl["seq"]: h.dma_start(out=xst[xb_], in_=xs_d[s][r0:r0 + 128, :]),
                 w=[("xst", 0)], dma=("xst", 0))
            P.op("dve", lambda h, xb_=xb_: h.memset(rst1[xb_][:, 0:1], 0.0), w=[("rst1a", xb_)])
            P.op("act", lambda h, xb_=xb_: h.activation(out=hn_tm[xb_], in_=xst[xb_], func=AF.Square, accum_out=rst1[xb_][:, 0:1]),
                 r=[("xst", 0)], w=[("hn_tm", 0), ("rst1a", xb_)])
            P.op("act", lambda h, xb_=xb_: h.activation(out=rst1[xb_][:, 1:2], in_=rst1[xb_][:, 0:1], func=AF.Sqrt, scale=1.0 / D, bias=epsb),
                 r=[("rst1a", xb_), "epsb"], w=[("rst1b", xb_)])
            P.op("dve", lambda h, xb_=xb_: h.reciprocal(out=rst1[xb_][:, 0:1], in_=rst1[xb_][:, 1:2]),
                 r=[("rst1b", xb_)], w=[("rst1a", xb_)])
            P.op("act", lambda h, xb_=xb_: h.activation(out=hn_tm[xb_], in_=xst[xb_], func=AF.Copy, scale=rst1[xb_][:, 0:1]),
                 r=[("xst", 0), ("rst1a", xb_)], w=[("hn_tm", 0)])

        def prep_tr(ti, ts):
            hb = ti % 2
            transposes_to(lambda kc: hn_tm[0][:, kc * 128:(kc + 1) * 128], [("hn_tm", 0)],
                          hnT[hb], ("hnT", hb), ts, gmix)

        def prep1(ti):
            prep_tables(ti)
            for ts in range(4):
                prep_norm(ti, ts)
                prep_tr(ti, ts)

        def qk_epi(tl, hb, cgi, is_q):
            qk_i = 0 if is_q else 1
            dst = (qT_s if is_q else kT_s)[tl["seq"]]
            h0 = (cgi % 4) * 4
            sb_ = cnt1["qk"] % 2
            cnt1["qk"] += 1
            stage = qstage[sb_]
            skey = ("qstage", 0)
            seq = tl["seq"]
            C1, S2, S1, C2 = rtab[qk_i]
            tk = ("rtab", qk_i)

            def epi_all(banks):
                R = range(4)
                for ts in R:
                    if ts < 2:
                        P.op("act", lambda h, ts=ts: h.activation(out=qn[ts].rearrange("p a d -> p (a d)"), in_=ps[banks[ts]][:], func=AF.Copy),
                             r=[("ps", banks[ts])], w=[("qn", ts)])
                    else:
                        P.op("dve", lambda h, ts=ts: h.tensor_copy(out=qn[ts].rearrange("p a d -> p (a d)"), in_=ps[banks[ts]][:]),
                             r=[("ps", banks[ts])], w=[("qn", ts)])
                for ts in R:
                    P.op("act", lambda h, ts=ts: h.activation(out=sqt[ts], in_=qn[ts].rearrange("p a d -> p (a d)"), func=AF.Square), r=[("qn", ts)], w=[("sqt", ts)])
                for ts in R:
                    P.op("dve", lambda h, ts=ts: h.tensor_reduce(out=ssq[ts][:, 0:4], in_=sqt[ts].rearrange("p (a d) -> p a d", d=128), axis=AX.X, op=ALU.add),
                         r=[("sqt", ts)], w=[("ssq", ts)])
                for ts in R:
                    P.op("act", lambda h, ts=ts: h.activation(out=ssq[ts][:, 4:8], in_=ssq[ts][:, 0:4], func=AF.Sqrt, scale=1.0 / HD, bias=epsb),
                         r=[("ssq", ts), "epsb"], w=[("ssqb", ts)])
                for ts in R:
                    P.op("dve", lambda h, ts=ts: h.reciprocal(out=ssq[ts][:, 0:4], in_=ssq[ts][:, 4:8]), r=[("ssqb", ts)], w=[("ssq", ts)])
                for ts in R:
                    P.op("dve", lambda h, ts=ts: h.tensor_tensor(out=qn[ts], in0=qn[ts],
                                                                 in1=bc(ssq[ts][:, 0:4].unsqueeze(2), [128, 4, 128]), op=ALU.mult),
                         r=[("qn", ts), ("ssq", ts)], w=[("qn", ts)])
                for (ta, tb_, half, op_) in ((C1, S2, 0, ALU.subtract), (S1, C2, 1, ALU.add)):
                    for ts in R:
                        P.op("dve", lambda h, ts=ts, ta=ta: h.tensor_tensor(out=rt[2 * ts], in0=qn[ts][:, :, 0:64], in1=bc(ta[:, ts, :].unsqueeze(1), [128, 4, 64]), op=ALU.mult),
                             r=[("qn", ts), tk], w=[("rt", 2 * ts)])
                    for ts in R:
                        P.op("dve", lambda h, ts=ts, tb_=tb_: h.tensor_tensor(out=rt[2 * ts + 1], in0=qn[ts][:, :, 64:128], in1=bc(tb_[:, ts, :].unsqueeze(1), [128, 4, 64]), op=ALU.mult),
                             r=[("qn", ts), tk], w=[("rt", 2 * ts + 1)])
                    for ts in R:
                        P.op("dve", lambda h, ts=ts, half=half, op_=op_: h.tensor_tensor(out=qo[ts][:, :, half * 64:(half + 1) * 64], in0=rt[2 * ts], in1=rt[2 * ts + 1], op=op_),
                             r=[("rt", 2 * ts), ("rt", 2 * ts + 1)], w=[("qo", ts)])

                def late():
                    tbs = [next_bank(), next_bank()]
                    for ts in R:
                        tb = tbs[ts // 2]
                        o0 = (ts % 2) * 512
                        for a_ in range(4):
                            P.op("pe", lambda h, a_=a_, ts=ts, tb=tb, o0=o0: h.transpose(psb[tb][:, o0 + a_ * 128:o0 + (a_ + 1) * 128], qo[ts][:, a_, :], ident),
                                 r=[("qo", ts), "ident"], w=[("ps", tb)])
                    for ts in R:
                        tb = tbs[ts // 2]
                        o0 = (ts % 2) * 512
                        eng_ = "act" if ts % 2 == 0 else "dve"
                        if eng_ == "act":
                            P.op("act", lambda h, ts=ts, tb=tb, o0=o0: h.activation(out=stage[:, :, ts * 128:(ts + 1) * 128],
                                                                                    in_=psb[tb][:, o0:o0 + 512].rearrange("p (a t) -> p a t", t=128), func=AF.Copy),
                                 r=[("ps", tb)], w=[skey])
                        else:
                            P.op("dve", lambda h, ts=ts, tb=tb, o0=o0: h.tensor_copy(out=stage[:, :, ts * 128:(ts + 1) * 128],
                                                                                     in_=psb[tb][:, o0:o0 + 512].rearrange("p (a t) -> p a t", t=128)),
                                 r=[("ps", tb)], w=[skey])
                    t0 = tl["t0"]
                    P.op("sp", lambda h: h.dma_start(out=dst[h0:h0 + 4, :, t0:t0 + 512].rearrange("a d t -> d a t"), in_=stage),
                         r=[skey], w=[("qk_s", seq)], dma=("qk_st", sb_))
                deferred.append(late)
            return epi_all

        def v_epi(tl, cgi):
            seq = tl["seq"]

            def epi(ts, b):
                i3 = cnt1["v"] % 2
                cnt1["v"] += 1
                r0 = tl["t0"] + ts * 128
                c0 = (cgi - 8) * 512
                P.op("act", lambda h: h.activation(out=vst[i3], in_=ps[b][:], func=AF.Copy), r=[("ps", b)], w=[("vst", i3)])
                P.op("sp", lambda h: h.dma_start(out=v_s[seq][r0:r0 + 128, c0:c0 + 512], in_=vst[i3]), r=[("vst", i3)], w=[("v_s", seq)], dma=("v_st", i3))
            return epi

        def glu_epi(tl, gi, NT):
            seq = tl["seq"]

            def epi(j, bA_, bB_):
                i2 = uid() % 2
                i3 = cnt1["u"] % 2
                cnt1["u"] += 1
                c = gi * 2 + j
                c0 = 15 + tl["t0"]
                P.op("act", lambda h: h.activation(out=sg[i2][:, 0:NT], in_=ps[bB_][:, 0:NT], func=AF.Sigmoid), r=[("ps", bB_)], w=[("sg", 0)])
                P.op("dve", lambda h: h.tensor_tensor(out=ust[i3][:, 0:NT], in0=ps[bA_][:, 0:NT], in1=sg[i2][:, 0:NT], op=ALU.mult),
                     r=[("ps", bA_), ("sg", 0)], w=[("ust", i3)])
                P.op("sp", lambda h: h.dma_start(out=uT_s[seq][c * 128:(c + 1) * 128, c0:c0 + NT], in_=ust[i3][:, 0:NT]),
                     r=[("ust", i3)], w=[("uT", seq)], dma=("uT_st", i3))
            return epi

        prep1(0)
        gen0 = conv_gen(0, acc1, uwin1, ysq1, next_bank, "p1")
        for ti, tl in enumerate(tiles1):
            hb = ti % 2
            groups = []
            for cgi in range(12):
                if cgi < 4 and not tl["q"]:
                    continue
                groups.append(("tm", cgi))
            if tl["conv"]:
                for gi in range(8):
                    groups.append(("fm", gi))
            lhs = lambda kg, ts, hb=hb: hnT[hb][:, kg, ts * 128:(ts + 1) * 128]
            for gidx, (kind, gi) in enumerate(groups):
                G_ = len(groups)
                if ti + 1 < len(tiles1):
                    k_ = gidx - (G_ - 5)
                    if k_ == 0:
                        prep_tables(ti + 1)
                    if 1 <= k_ <= 4:
                        prep_tr(ti + 1, k_ - 1)
                    if 0 <= k_ <= 3:
                        prep_norm(ti + 1, k_)
                if ti >= 2 and not (kind == "tm" and gi < 8):
                    drive(gen0, 8)
                if kind == "tm":
                    wb = [(w_in_v[:, kb * 8:(kb + 1) * 8, gi * 512:(gi + 1) * 512], 8) for kb in range(4)]
                    if gi < 8:
                        tm_group(wb, lhs, [("hnT", hb)], None, epi_all=qk_epi(tl, hb, gi, gi < 4))
                    else:
                        tm_group(wb, lhs, [("hnT", hb)], v_epi(tl, gi))
                else:
                    NT = tl["conv"]
                    ca = 3 * AW + gi * 256
                    cg_ = 3 * AW + CW + gi * 256
                    wA = [w_in_v[:, kb * 16:(kb + 1) * 16, ca:ca + 256] for kb in range(2)]
                    wB = [w_in_v[:, kb * 16:(kb + 1) * 16, cg_:cg_ + 256] for kb in range(2)]
                    fm_pair(wA, wB, lambda kg, hb=hb, NT=NT: hnT[hb][:, kg, 0:NT], [("hnT", hb)], NT, glu_epi(tl, gi, NT))
            while deferred:
                deferred.pop(0)()
        drain(gen0)
        ln_finalize(acc1[0], acc1[1], ("p1acc", 0), ("p1acc", 1))

        P.barrier()
        AR.reset(base_mark)

        maskM = AR.alloc([128, MASKW], BF16)
        P.op("pool", lambda h: h.dma_start(out=maskM, in_=mask_d), w=["maskM"], dma="maskld")
        kT = [AR.alloc([128, S], BF16) for _ in range(3)]
        vh = [AR.alloc([128, 16, 128], BF16) for _ in range(3)]
        qT = [AR.alloc([128, S], BF16) for _ in range(3)]
        NPT = 8
        LOOK = 6
        pt = [AR.alloc([128, 512], BF16) for _ in range(NPT)]
        rl = [AR.alloc([128, 512], F32) for _ in range(2)]
        ast = [AR.alloc([128, 512], BF16) for _ in range(2)]
        SB = [0, 1, 2]
        OB = [3, 4]
        LB = [5, 6]
        scale = float(HD) ** -0.5
        heads = [(seq, hh) for seq in range(2) for hh in range(NH)]

        def load_head(n):
            seq, hh = heads[n]
            i = n % 3
            tq = TQ[seq]
            P.op("sp", lambda h: h.dma_start(out=kT[i], in_=kT_s[seq][hh]), r=[("qk_s", seq)], w=[("kT", i)], dma=("kT", i))
            P.op("sp", lambda h: h.dma_start(out=vh[i], in_=v_s[seq][:, hh * 128:(hh + 1) * 128].rearrange("(kb p) d -> p kb d", p=128)),
                 r=[("v_s", seq)], w=[("vh", i)], dma=("vh", i))
            P.op("sp", lambda h: h.dma_start(out=qT[i][:, 0:tq], in_=qT_s[seq][hh]), r=[("qk_s", seq)], w=[("qT", i)], dma=("qT", i))

        steps = []
        qbc = 0
        for n, (seq, hh) in enumerate(heads):
            for qb in range(TQ[seq] // 512):
                q0 = qb * 512
                kbs = [kb for kb in range(16) if kb * 128 + 127 >= q0 - 1024 and kb * 128 <= q0 + 511 + 1024]
                oi = qbc % 2
                qbc += 1
                for kb in kbs:
                    steps.append(dict(n=n, i=n % 3, seq=seq, hh=hh, q0=q0, kb=kb, first=(kb == kbs[0]), last=(kb == kbs[-1]),
                                      oi=oi, head_first=(qb == 0 and kb == kbs[0])))
        ctr = {"s": 0, "pt": 0}

        def emit_qk(sp_):
            i, q0, kb = sp_["i"], sp_["q0"], sp_["kb"]
            sbk = SB[ctr["s"] % 3]
            ctr["s"] += 1
            pi = ctr["pt"] % NPT
            ctr["pt"] += 1
            mm(ps[sbk][:], kT[i][:, kb * 128:(kb + 1) * 128], qT[i][:, q0:q0 + 512], True, True,
               [("kT", i), ("qT", i)], [("ps", sbk)])
            P.op("act", lambda h: h.activation(out=pt[pi], in_=ps[sbk][:], func=AF.Exp, scale=scale, bias=negB),
                 r=[("ps", sbk), "negB"], w=[("pt", pi)])
            off = U0 - (kb * 128 - q0)
            P.op("dve", lambda h: h.tensor_tensor(out=pt[pi], in0=pt[pi], in1=maskM[:, off:off + 512], op=ALU.mult),
                 r=[("pt", pi), "maskM"], w=[("pt", pi)])
            return pi

        def emit_pv(sp_, pi):
            i, kb, oi = sp_["i"], sp_["kb"], sp_["oi"]
            ob, lbk = OB[oi], LB[oi]
            mm(ps[ob][:], vh[i][:, kb, :], pt[pi], sp_["first"], sp_["last"], [("vh", i), ("pt", pi)], [("ps", ob)])
            mm(ps[lbk][:], onesb, pt[pi], sp_["first"], sp_["last"], ["onesb", ("pt", pi)], [("ps", lbk)])
            if sp_["last"]:
                seq, hh, q0 = sp_["seq"], sp_["hh"], sp_["q0"]
                P.op("dve", lambda h: h.reciprocal(out=rl[oi], in_=ps[lbk][:]), r=[("ps", lbk)], w=[("rl", oi)])
                P.op("dve", lambda h: h.tensor_tensor(out=ast[oi], in0=ps[ob][:], in1=rl[oi], op=ALU.mult),
                     r=[("ps", ob), ("rl", oi)], w=[("ast", oi)])
                P.op("sp", lambda h: h.dma_start(out=attnT_s[seq][hh * 128:(hh + 1) * 128, q0:q0 + 512], in_=ast[oi]),
                     r=[("ast", oi)], w=[("attn_s", seq)], dma=("attn_st", oi))

        load_head(0)
        load_head(1)
        pend = []
        for sp_ in steps:
            if sp_["head_first"] and sp_["n"] >= 1 and sp_["n"] + 1 < len(heads):
                load_head(sp_["n"] + 1)
            pend.append((sp_, emit_qk(sp_)))
            if len(pend) > LOOK:
                a_, b_ = pend.pop(0)
                emit_pv(a_, b_)
        while pend:
            a_, b_ = pend.pop(0)
            emit_pv(a_, b_)

        P.barrier()
        AR.reset(base_mark)
        st["bank"] = 0

        R1 = AR.alloc([128, NFF, 512], BF16)
        r1_off = AR.off - NFF * 512 * 2
        hbuf = arena_t[:, r1_off // 2:(r1_off + 4 * D * 4) // 2].bitcast(F32).rearrange("p (a b) -> p a b", b=D)
        tail = r1_off + 4 * D * 4
        hf_tm = [arena_t[:, (tail + i * D * 2) // 2:(tail + (i + 1) * D * 2) // 2] for i in range(2)]
        assert tail + 2 * D * 2 <= r1_off + NFF * 512 * 2
        X0 = AR.alloc([128, 32, 512], BF16)
        inst = [AR.alloc([128, 512], F32) for _ in range(4)]
        ost = [AR.alloc([128, 512], F32) for _ in range(3)]
        uwin = [AR.alloc([128, 544], F32) for _ in range(4)]
        sgt = [AR.alloc([128, 512], F32) for _ in range(2)]
        acc = [AR.alloc([128, 512], F32) for _ in range(4)]
        ysq = [AR.alloc([128, 512], F32) for _ in range(2)]
        sqj = AR.alloc([128, 512], BF16)
        X0ALL = ["X0"] + [("X0c", c) for c in range(16)]
        ssp = AR.alloc([128, 4, 8], F32)
        rs4 = AR.alloc([128, 8], F32)

        print("phase3 arena", AR.off)
        c3 = {"in": 0, "o": 0, "uw": 0}

        S1B, S2B = 6, 7

        def next_bank6():
            b = st["bank"]
            st["bank"] = (b + 1) % 6
            return b

        def conv_hooks_r3(tidx):
            seq, t0 = tiles3[tidx]
            cwt = cw[seq]
            cwk = "cw%d" % seq

            def conv_pair(ca, cb_):
                par = c3["uw"] % 2
                c3["uw"] += 1
                chains = []
                for n_, c in enumerate((ca, cb_)):
                    ui = 2 * par + n_
                    P.op("sp", lambda h, ui=ui, c=c: h.dma_start(out=uwin[ui][:, 0:542], in_=uT_s[seq][c * 128:(c + 1) * 128, t0:t0 + 542]),
                         r=[("uT", seq)], w=[("p3uwin", ui)], dma=("p3uwin", ui))
                    chains.append((c, ui, 2 * n_))
                for (c, ui, a0) in chains:
                    P.op("dve", lambda h, c=c, ui=ui, a0=a0: h.tensor_scalar(out=acc[a0], in0=uwin[ui][:, 0:512], scalar1=cwt[:, c, 0:1], scalar2=cb[:, c:c + 1], op0=ALU.mult, op1=ALU.add),
                         r=[("p3uwin", ui), cwk, "cb"], w=[("p3acc", a0)])
                cur = 0
                for tap in range(1, TAPS):
                    nxt = 1 - cur
                    for (c, ui, a0) in chains:
                        P.op("dve", lambda h, tap=tap, cur=cur, nxt=nxt, c=c, ui=ui, a0=a0: h.scalar_tensor_tensor(
                            out=acc[a0 + nxt], in0=uwin[ui][:, tap:tap + 512], scalar=cwt[:, c, tap:tap + 1], in1=acc[a0 + cur], op0=ALU.mult, op1=ALU.add),
                            r=[("p3uwin", ui), cwk, ("p3acc", a0 + cur)], w=[("p3acc", a0 + nxt)])
                    cur = nxt
                for n_, (c, ui, a0) in enumerate(chains):
                    fin = a0 + cur
                    P.op("act", lambda h, fin=fin, n_=n_: h.activation(out=ysq[n_], in_=acc[fin], func=AF.Square), r=[("p3acc", fin)], w=[("p3ysq", n_)])
                    P.op("act", lambda h, fin=fin, c=c: h.activation(out=X0[:, 16 + c, :], in_=acc[fin], func=AF.Copy), r=[("p3acc", fin)], w=[("X0c", c)])
                for n_, (c, ui, a0) in enumerate(chains):
                    fin = a0 + cur
                    mm(ps[S1B][:], onesf, acc[fin], c == 0, c == 15, ["onesf", ("p3acc", fin)], [("ps", S1B)])
                    mm(ps[S2B][:], onesf, ysq[n_], c == 0, c == 15, ["onesf", ("p3ysq", n_)], [("ps", S2B)])

            def finalize():
                P.op("dve", lambda h: h.tensor_single_scalar(out=s1acc, in_=ps[S1B][:], scalar=1.0 / CW, op=ALU.mult), r=[("ps", S1B)], w=["s1acc"])
                P.op("dve", lambda h: h.tensor_tensor(out=acc[0], in0=s1acc, in1=s1acc, op=ALU.mult), r=["s1acc"], w=[("p3acc", 0)])
                P.op("dve", lambda h: h.scalar_tensor_tensor(out=acc[1], in0=ps[S2B][:], scalar=1.0 / CW, in1=acc[0], op0=ALU.mult, op1=ALU.subtract),
                     r=[("ps", S2B), ("p3acc", 0)], w=[("p3acc", 1)])
                P.op("act", lambda h: h.activation(out=acc[0], in_=acc[1], func=AF.Sqrt, bias=epsb), r=[("p3acc", 1), "epsb"], w=[("p3acc", 0)])
                P.op("dve", lambda h: h.reciprocal(out=lnr, in_=acc[0]), r=[("p3acc", 0)], w=["lnr"])
                P.op("dve", lambda h: h.scalar_tensor_tensor(out=lnb, in0=s1acc, scalar=-1.0, in1=lnr, op0=ALU.mult, op1=ALU.mult), r=["s1acc", "lnr"], w=["lnb"])

            def norm4(c0):
                cs = range(c0, c0 + 4)
                for c in cs:
                    i4 = c % 4
                    P.op("dve", lambda h, c=c, i4=i4: h.tensor_tensor(out=acc[i4], in0=X0[:, 16 + c, :], in1=lnr, op=ALU.mult), r=[("X0c", c), "lnr"], w=[("p3acc", i4)])
                for c in cs:
                    i4 = c % 4
                    P.op("dve", lambda h, c=c, i4=i4: h.tensor_tensor(out=acc[i4], in0=acc[i4], in1=lnb, op=ALU.add), r=[("p3acc", i4), "lnb"], w=[("p3acc", i4)])
                for c in cs:
                    i4 = c % 4
                    P.op("act", lambda h, c=c, i4=i4: h.activation(out=X0[:, 16 + c, :], in_=acc[i4], func=AF.Silu, scale=lg[:, c:c + 1], bias=lb[:, c:c + 1]),
                         r=[("p3acc", i4), "lg", "lb"], w=[("X0c", c)])

            return [(lambda: (attn_load(tidx), conv_pair(0, 1))),
                    (lambda: (conv_pair(2, 3), conv_pair(4, 5))),
                    (lambda: conv_pair(6, 7)),
                    (lambda: (conv_pair(8, 9), conv_pair(10, 11))),
                    (lambda: conv_pair(12, 13)),
                    (lambda: (conv_pair(14, 15), finalize())),
                    (lambda: (norm4(0), norm4(4))),
                    (lambda: (norm4(8), norm4(12)))]

        def normalize4(tidx, c0):
            cs = range(c0, c0 + 4)
            for c in cs:
                i4 = c % 4
                P.op("sp", lambda h, c=c, i4=i4: h.dma_start(out=acc[i4], in_=y_s[tidx, c * 128:(c + 1) * 128, :]), r=[("y_s", tidx)], w=[("p3acc", i4)], dma=("yld", i4))
            for c in cs:
                i4 = c % 4
                P.op("dve", lambda h, c=c, i4=i4: h.tensor_tensor(out=acc[i4], in0=acc[i4], in1=lnr, op=ALU.mult), r=[("p3acc", i4), "lnr"], w=[("p3acc", i4)])
            for c in cs:
                i4 = c % 4
                P.op("dve", lambda h, c=c, i4=i4: h.tensor_tensor(out=acc[i4], in0=acc[i4], in1=lnb, op=ALU.add), r=[("p3acc", i4), "lnb"], w=[("p3acc", i4)])
            for c in cs:
                i4 = c % 4
                P.op("act", lambda h, c=c, i4=i4: h.activation(out=X0[:, 16 + c, :], in_=acc[i4], func=AF.Silu, scale=lg[:, c:c + 1], bias=lb[:, c:c + 1]),
                     r=[("p3acc", i4), "lg", "lb"], w=[("X0c", c)])

        def attn_load(tidx):
            seq, t0 = tiles3[tidx]
            P.op("sp", lambda h: h.dma_start(out=X0[:, 0:16, :], in_=attnT_s[seq][:, t0:t0 + 512].rearrange("(c p) t -> p c t", p=128)),
                 r=[("attn_s", seq)], w=["X0"], dma="X0ld")

        def outproj(ti, seq, t0):
            def epi_for(cg):
                def epi(ts, b):
                    ii = c3["in"] % 4
                    c3["in"] += 1
                    r0 = t0 + ts * 128
                    P.op("sp", lambda h: h.dma_start(out=inst[ii], in_=xs_d[seq][r0:r0 + 128, cg * 512:(cg + 1) * 512]), w=[("inst", ii)], dma=("inst", ii))
                    P.op("dve", lambda h: h.tensor_tensor(out=hbuf[:, ts, cg * 512:(cg + 1) * 512], in0=ps[b][:], in1=inst[ii], op=ALU.add),
                         r=[("ps", b), ("inst", ii)], w=["R1"])
                    P.op("act", lambda h: h.activation(out=sqj, in_=hbuf[:, ts, cg * 512:(cg + 1) * 512], func=AF.Square, accum_out=ssp[:, ts, cg:cg + 1]),
                         r=["R1"], w=["sqj", "ssp"])
                return epi
            P.op("dve", lambda h: h.memset(ssp, 0.0), w=["ssp"])
            for cg in range(8):
                wb = [(w_out_v[:, kb * 8:(kb + 1) * 8, cg * 512:(cg + 1) * 512], 8) for kb in range(4)]
                tm_group6(wb, lambda kg, ts: X0[:, kg, ts * 128:(ts + 1) * 128], X0ALL, epi_for(cg))
            P.op("dve", lambda h: h.tensor_reduce(out=rs4[:, 0:4], in_=ssp, axis=AX.X, op=ALU.add), r=["ssp"], w=["rs4a"])
            P.op("act", lambda h: h.activation(out=rs4[:, 4:8], in_=rs4[:, 0:4], func=AF.Sqrt, scale=1.0 / D, bias=epsb), r=["rs4a", "epsb"], w=["rs4b"])
            P.op("dve", lambda h: h.reciprocal(out=rs4[:, 0:4], in_=rs4[:, 4:8]), r=["rs4b"], w=["rs4a"])
            for ts in range(4):
                fi = ts % 2
                P.op("sp", lambda h, ts=ts: h.dma_start(out=h_s[ti, ts * 128:(ts + 1) * 128, :], in_=hbuf[:, ts, :]), r=["R1"], w=["h_s"], dma="hs_st")
                P.op("act", lambda h, ts=ts, fi=fi: h.activation(out=hf_tm[fi], in_=hbuf[:, ts, :], func=AF.Copy, scale=rs4[:, ts:ts + 1]),
                     r=["R1", "rs4a"], w=[("hf_tm", fi)])
                transposes_to6(lambda kc, fi=fi: hf_tm[fi][:, kc * 128:(kc + 1) * 128], [("hf_tm", fi)], X0, X0ALL, ts, gffn)

        def tm_group6(wblocks, lhs, lhs_keys, epi, hook=None):
            banks = [next_bank6() for _ in range(4)]
            nkt = sum(nk for _, nk in wblocks)
            kg0 = 0
            for dap, nk in wblocks:
                view, wkey = wload(dap, nk, 512)
                for ts in range(4):
                    for k in range(nk):
                        kg = kg0 + k
                        mm(ps[banks[ts]][:], lhs(kg, ts), view[:, k, :], kg == 0, kg == nkt - 1,
                           [wkey] + lhs_keys, [("ps", banks[ts])])
                kg0 += nk
            if hook is not None:
                hook()
            for ts in range(4):
                epi(ts, banks[ts])

        def transposes_to6(src_fn, src_keys, dst, dst_key, ts, gcol):
            for g in range(4):
                b = next_bank6()
                for k in range(8):
                    kc = g * 8 + k
                    P.op("pe", lambda h, b=b, k=k, kc=kc: h.transpose(psb[b][:, k * 128:(k + 1) * 128], src_fn(kc), ident),
                         r=src_keys + ["ident"], w=[("ps", b)])
                P.op("dve", lambda h, b=b, g=g: h.tensor_tensor(
                    out=dst[:, g * 8:(g + 1) * 8, ts * 128:(ts + 1) * 128],
                    in0=psb[b].rearrange("p (k t) -> p k t", t=128),
                    in1=bc(gcol[:, g * 8:(g + 1) * 8].unsqueeze(2), [128, 8, 128]), op=ALU.mult),
                    r=[("ps", b), "gmix", "gffn"], w=list(dst_key))

        def gateup(gen=None):
            for gi in range(NFF // 2):
                if gi >= 1:
                    drive(gen, 8)
                c0 = gi * 256
                wA = [w_gate_v[:, kb * 16:(kb + 1) * 16, c0:c0 + 256] for kb in range(2)]
                wB = [w_up_v[:, kb * 16:(kb + 1) * 16, c0:c0 + 256] for kb in range(2)]
                bA = [next_bank6(), next_bank6()]
                bB = [next_bank6(), next_bank6()]
                for kb2 in range(2):
                    for blocks, banks in ((wA, bA), (wB, bB)):
                        view, wkey = wload(blocks[kb2], 16, 256)
                        for j in range(2):
                            for k in range(16):
                                kg = kb2 * 16 + k
                                mm(ps[banks[j]][:], view[:, k, j * 128:(j + 1) * 128], X0[:, kg, :], kg == 0, kg == 31,
                                   [wkey] + X0ALL, [("ps", banks[j])])
                for j in range(2):
                    i2 = uid() % 2
                    ffc = gi * 2 + j
                    P.op("act", lambda h, b=bA[j], i2=i2: h.activation(out=sgt[i2], in_=ps[b][:], func=AF.Silu), r=[("ps", bA[j])], w=[("sgt", i2)])
                    P.op("dve", lambda h, b=bB[j], i2=i2, ffc=ffc: h.tensor_tensor(out=R1[:, ffc, :], in0=ps[b][:], in1=sgt[i2], op=ALU.mult),
                         r=[("ps", bB[j]), ("sgt", i2)], w=["R1"])

        def down(ti, seq, t0, hooks):
            for cg in range(8):
                wb = []
                for fb in range(11):
                    nk = 8 if fb < 10 else 6
                    wb.append((w_down_v[:, fb * 8:fb * 8 + nk, cg * 512:(cg + 1) * 512], nk))

                def epi(ts, b, cg=cg):
                    ii = c3["in"] % 4
                    c3["in"] += 1
                    oi = c3["o"] % 3
                    c3["o"] += 1
                    r0 = t0 + ts * 128
                    P.op("sp", lambda h: h.dma_start(out=inst[ii], in_=h_s[ti, ts * 128:(ts + 1) * 128, cg * 512:(cg + 1) * 512]), r=["h_s"], w=[("inst", ii)], dma=("inst", ii))
                    P.op("dve", lambda h: h.tensor_tensor(out=ost[oi], in0=ps[b][:], in1=inst[ii], op=ALU.add), r=[("ps", b), ("inst", ii)], w=[("ost", oi)])
                    P.op("sp", lambda h: h.dma_start(out=ys_d[seq][r0:r0 + 128, cg * 512:(cg + 1) * 512], in_=ost[oi]), r=[("ost", oi)], dma=("out", oi))
                tm_group6(wb, lambda kg, ts: R1[:, kg, ts * 128:(ts + 1) * 128], ["R1"], epi, hook=(hooks[cg] if hooks else None))

        attn_load(0)
        for c0 in (0, 4, 8, 12):
            normalize4(0, c0)
        for ti, (seq, t0) in enumerate(tiles3):
            outproj(ti, seq, t0)
            hooks = None
            gen = None
            if ti + 1 < len(tiles3):
                hooks = conv_hooks_r3(ti + 1)
            gateup(gen)
            down(ti, seq, t0, hooks)

        fin = [("out", 0), ("out", 1), ("out", 2)]
        if DEBUG:
            fin = list(P.dma_cnt.keys())
        P.emit(nc, es, final_wait_keys=fin)
    return nc


def _mask_table():
    i = np.arange(128)[:, None]
    u = np.arange(MASKW)[None, :]
    dlt = np.abs(i - u + U0)
    c = (dlt <= 64).astype(np.float32)
    c += ((dlt % 4 == 0) & (dlt <= 256)).astype(np.float32)
    c += ((dlt % 16 == 0) & (dlt <= 1024)).astype(np.float32)
    return np.ascontiguousarray(c, dtype=np.float32)


def _rope_tables(rev):
    half = HD // 2
    freqs = (10000.0 ** (-np.arange(half, dtype=np.float32) * 2.0 / HD)).astype(np.float32)
    pos = np.arange(S, dtype=np.float32)
    if rev:
        pos = pos[::-1]
    ang = (pos[:, None] * freqs[None, :]).astype(np.float32)
    cos = np.cos(ang).astype(np.float32).reshape(16, 128, 64).transpose(1, 0, 2)
    sin = np.sin(ang).astype(np.float32).reshape(16, 128, 64).transpose(1, 0, 2)
    return np.ascontiguousarray(cos), np.ascontiguousarray(sin)


def _col(v, n):
    return np.ascontiguousarray(np.asarray(v, np.float32).reshape(n, 128).T)


_NC_CACHE = {}


def make_in_maps(x_prompt, x_sample, norm_mix_g, w_in, q_norm_g, k_norm_g, conv_w, conv_b, conv_ln_g, conv_ln_b,
                 w_out, norm_ffn_g, w_gate, w_up, w_down):
    f = lambda a: np.ascontiguousarray(np.asarray(a, dtype=np.float32))
    x_prompt = f(x_prompt); x_sample = f(x_sample)
    shared = dict(
        w_in=f(w_in[0]), w_out=f(w_out[0]), w_gate=f(w_gate[0]), w_up=f(w_up[0]), w_down=f(w_down[0]),
        gmix=_col(norm_mix_g[0], 32), gffn=_col(norm_ffn_g[0], 32),
        gq=np.ascontiguousarray(np.broadcast_to(f(q_norm_g[0])[None, :], (128, 128))),
        gk=np.ascontiguousarray(np.broadcast_to(f(k_norm_g[0])[None, :], (128, 128))),
        cb=_col(conv_b[0], 16), lg=_col(conv_ln_g[0], 16), lb=_col(conv_ln_b[0], 16),
        maskm=_mask_table(),
    )
    cwn = f(conv_w[0])
    cw_f = np.ascontiguousarray(cwn.reshape(TAPS, 16, 128).transpose(2, 1, 0))
    cw_r = np.ascontiguousarray(cwn[::-1].reshape(TAPS, 16, 128).transpose(2, 1, 0))
    cos_f, sin_f = _rope_tables(False)
    cos_r, sin_r = _rope_tables(True)
    in_maps = []
    for c in range(8):
        odd = c % 2 == 1
        xb = x_sample[c // 2]
        if odd:
            xb = np.ascontiguousarray(xb[::-1])
        m = dict(shared)
        m.update(xa=x_prompt[c], xb=xb, cwa=cw_f, cwb=cw_r if odd else cw_f,
                 cosa=cos_f, sina=sin_f, cosb=cos_r if odd else cos_f, sinb=sin_r if odd else sin_f)
        in_maps.append(m)
    return in_maps


def kernel(**inputs):
    in_maps = make_in_maps(**inputs)
    if "nc" not in _NC_CACHE:
        _NC_CACHE["nc"] = build_program()
    nc = _NC_CACHE["nc"]
    res = run_bass_kernel_spmd(nc, in_maps, core_ids=list(range(8)))
    y_prompt = np.empty((8, S, D), np.float32)
    y_sample = np.empty((4, S, D), np.float32)
    for c in range(8):
        r = res.results[c]
        y_prompt[c] = r["ya"]
        yb = np.asarray(r["yb"], np.float32)
        if c % 2 == 0:
            y_sample[c // 2, 0:1024] = yb
        else:
            y_sample[c // 2, 1024:2048] = yb[::-1]
    if DEBUG:
        kernel.last = res
    return (y_prompt, y_sample)
```

```python
import contextlib
import numpy as np
import concourse.bass as bass
import concourse.mybir as mybir
from concourse.bass_utils import run_bass_kernel_spmd

F32 = mybir.dt.float32
BF16 = mybir.dt.bfloat16
ALU = mybir.AluOpType
AF = mybir.ActivationFunctionType
AX = mybir.AxisListType

D = 4096
S = 2048
NH = 16
HD = 128
AW = 2048
CW = 2048
TAPS = 31
DFF = 11008
NFF = DFF // 128
INC = 10240
EPS = 1e-6
NSLOT = 4
U0 = 1536
MASKW = 3968
DEBUG = False

ENGS = ("pe", "act", "dve", "pool", "sp")
SEM_CHUNK = 30000


class Ins:
    __slots__ = ("eng", "fn", "deps", "dmadeps", "sig", "cnt", "dma_key", "idx")


class Prog:
    def __init__(self):
        self.ins = []
        self.lastw = {}
        self.readers = {}
        self.dma_cnt = {}
        self.per_eng = {e: [] for e in ENGS}
        self.pending = {}

    def op(self, eng, fn, r=(), w=(), dma=None):
        i = Ins()
        i.eng = eng
        i.fn = fn
        i.dma_key = dma
        i.sig = False
        i.cnt = None
        deps = {}
        dmadeps = {}

        def add(d, raw):
            if d.dma_key is not None:
                dmadeps[d.dma_key] = self.dma_cnt[d.dma_key]
                return
            if d.eng == eng:
                if eng == "pe" or not raw:
                    return
            cur = deps.get(d.eng)
            if cur is None or d.idx > cur.idx:
                deps[d.eng] = d

        for k in r:
            lw = self.lastw.get(k)
            if lw is not None:
                add(lw, True)
        for k in w:
            lw = self.lastw.get(k)
            if lw is not None:
                add(lw, False)
            for rd in self.readers.get(k, {}).values():
                add(rd, False)
        pend = self.pending.pop(eng, None)
        if pend is not None:
            for d in pend[0]:
                if d.eng != eng:
                    cur = deps.get(d.eng)
                    if cur is None or d.idx > cur.idx:
                        deps[d.eng] = d
            for k, v in pend[1].items():
                dmadeps[k] = max(dmadeps.get(k, 0), v)
        i.deps = list(deps.values())
        i.dmadeps = dmadeps
        i.idx = len(self.ins)
        self.ins.append(i)
        self.per_eng[eng].append(i)
        if dma is not None:
            self.dma_cnt[dma] = self.dma_cnt.get(dma, 0) + 1
        rk = ("dma", dma) if dma is not None else eng
        for k in r:
            self.readers.setdefault(k, {})[rk] = i
        for k in w:
            self.lastw[k] = i
            self.readers[k] = {}
        return i

    def barrier(self):
        lasts = []
        for e in ENGS:
            for i in reversed(self.per_eng[e]):
                if i.dma_key is None:
                    lasts.append(i)
                    break
        snap = dict(self.dma_cnt)
        for e in ENGS:
            self.pending[e] = (list(lasts), dict(snap))

    def emit(self, nc, es, final_wait_keys=()):
        for i in self.ins:
            for d in i.deps:
                d.sig = True
        nsig = {}
        for e in ENGS:
            c = 0
            for i in self.per_eng[e]:
                if i.sig:
                    c += 1
                    i.cnt = c
            nsig[e] = c
        esem = {}
        for e in ENGS:
            n = max(1, -(-nsig[e] // SEM_CHUNK))
            esem[e] = [es.enter_context(nc.semaphore(f"s_{e}_{j}")) for j in range(n)]
        dsem = {}
        for n_, k in enumerate(self.dma_cnt):
            dsem[k] = es.enter_context(nc.semaphore(f"d_{n_}"))
        block = es.enter_context(nc.Block())
        prog = self

        def run_engine(e, h):
            waited = {}
            for i in prog.per_eng[e]:
                for d in i.deps:
                    c = d.cnt - 1
                    ch = c // SEM_CHUNK
                    v = c % SEM_CHUNK + 1
                    done = False
                    for (de, ch2), wv in waited.items():
                        if de == d.eng and (ch2 > ch or (ch2 == ch and wv >= v)):
                            done = True
                            break
                    if not done:
                        h.wait_ge(esem[d.eng][ch], v)
                        waited[(d.eng, ch)] = v
                for k, cntv in i.dmadeps.items():
                    kk = ("dma", k)
                    v = 16 * cntv
                    if waited.get(kk, 0) < v:
                        h.wait_ge(dsem[k], v)
                        waited[kk] = v
                bi = i.fn(h)
                if i.dma_key is not None:
                    bi.then_inc(dsem[i.dma_key], 16)
                elif i.sig:
                    c = i.cnt - 1
                    bi.then_inc(esem[e][c // SEM_CHUNK], 1)
            if e == "sp":
                for k in final_wait_keys:
                    h.wait_ge(dsem[k], 16 * prog.dma_cnt[k])

        @block.tensor
        def _(h):
            run_engine("pe", h)

        @block.scalar
        def _(h):
            run_engine("act", h)

        @block.vector
        def _(h):
            run_engine("dve", h)

        @block.gpsimd
        def _(h):
            run_engine("pool", h)

        @block.sync
        def _(h):
            run_engine("sp", h)


class Arena:
    def __init__(self, tensor, nbytes):
        self.t = tensor
        self.n = nbytes
        self.off = 0

    def alloc(self, shape, dt):
        esz = 4 if dt == F32 else 2
        free = 1
        for s_ in shape[1:]:
            free *= s_
        nb = free * esz
        self.off = (self.off + 31) // 32 * 32
        assert self.off + nb <= self.n, ("SBUF arena overflow", self.off, nb, self.n)
        a = self.t[:, self.off // 2:(self.off + nb) // 2]
        self.off += nb
        if dt == F32:
            a = a.bitcast(F32)
        if len(shape) == 3:
            a = a.rearrange("p (a b) -> p a b", b=shape[2])
        return a

    def mark(self):
        return self.off

    def reset(self, m):
        self.off = m


def bc(ap, shape):
    return ap.broadcast_to(shape)


def build_program():
    nc = bass.Bass("TRN2", target_bir_lowering=False)
    dt_in = lambda name, shape: nc.dram_tensor(name, shape, F32, kind="ExternalInput").ap()
    xs_d = [dt_in("xa", [S, D]), dt_in("xb", [S, D])]
    w_in = dt_in("w_in", [D, INC])
    w_out = dt_in("w_out", [D, D])
    w_gate = dt_in("w_gate", [D, DFF])
    w_up = dt_in("w_up", [D, DFF])
    w_down = dt_in("w_down", [DFF, D])
    gmix_d = dt_in("gmix", [128, 32])
    gffn_d = dt_in("gffn", [128, 32])
    gq_d = dt_in("gq", [128, 128])
    gk_d = dt_in("gk", [128, 128])
    cw_d = [dt_in("cwa", [128, 16, TAPS]), dt_in("cwb", [128, 16, TAPS])]
    cb_d = dt_in("cb", [128, 16])
    lg_d = dt_in("lg", [128, 16])
    lb_d = dt_in("lb", [128, 16])
    cos_d = [dt_in("cosa", [128, 16, 64]), dt_in("cosb", [128, 16, 64])]
    sin_d = [dt_in("sina", [128, 16, 64]), dt_in("sinb", [128, 16, 64])]
    mask_d = dt_in("maskm", [128, MASKW])
    ys_d = [nc.dram_tensor("ya", [S, D], F32, kind="ExternalOutput").ap(),
            nc.dram_tensor("yb", [1024, D], F32, kind="ExternalOutput").ap()]
    skind = "ExternalOutput" if DEBUG else "Internal"
    TQ = [2048, 1024]
    UW = [15 + 2048 + 15, 15 + 1152]
    qT_s = [nc.dram_tensor(f"qT_s{s}", [NH, 128, TQ[s]], BF16, kind=skind).ap() for s in range(2)]
    kT_s = [nc.dram_tensor(f"kT_s{s}", [NH, 128, S], BF16, kind=skind).ap() for s in range(2)]
    v_s = [nc.dram_tensor(f"v_s{s}", [S, AW], BF16, kind=skind).ap() for s in range(2)]
    uT_s = [nc.dram_tensor(f"uT_s{s}", [CW, UW[s]], F32, kind=skind).ap() for s in range(2)]
    attnT_s = [nc.dram_tensor(f"attnT_s{s}", [AW, TQ[s]], BF16, kind=skind).ap() for s in range(2)]
    h_s = nc.dram_tensor("h_s", [6, 512, D], F32, kind=skind).ap()
    y_s = nc.dram_tensor("y_s", [6, CW, 512], F32, kind=skind).ap()

    w_in_v = w_in.rearrange("(kc p) n -> p kc n", p=128)
    w_out_v = w_out.rearrange("(kc p) n -> p kc n", p=128)
    w_gate_v = w_gate.rearrange("(kc p) n -> p kc n", p=128)
    w_up_v = w_up.rearrange("(kc p) n -> p kc n", p=128)
    w_down_v = w_down.rearrange("(kc p) n -> p kc n", p=128)

    P = Prog()
    es = contextlib.ExitStack()
    with es:
        ARENA_BYTES = 207 * 1024
        arena_t = es.enter_context(nc.sbuf_tensor("arena", [128, ARENA_BYTES // 2], BF16))
        AR = Arena(arena_t, ARENA_BYTES)
        ps = [es.enter_context(nc.psum_tensor(f"ps{i}", [128, 512], F32)) for i in range(8)]
        psb = [p_[:].bitcast(BF16) for p_ in ps]
        st = {"bank": 0, "slot": 0, "n": 0}

        def next_bank():
            b = st["bank"]
            st["bank"] = (b + 1) % 8
            return b

        def uid():
            st["n"] += 1
            return st["n"]

        ident = AR.alloc([128, 128], BF16)
        identf = AR.alloc([128, 128], F32)
        onesf = AR.alloc([128, 128], F32)
        onesb = AR.alloc([128, 128], BF16)
        gmix = AR.alloc([128, 32], F32)
        gffn = AR.alloc([128, 32], F32)
        gq = AR.alloc([128, 128], F32)
        gk = AR.alloc([128, 128], F32)
        cw = [AR.alloc([128, 16, TAPS], F32) for _ in range(2)]
        cb = AR.alloc([128, 16], F32)
        lg = AR.alloc([128, 16], F32)
        lb = AR.alloc([128, 16], F32)
        epsb = AR.alloc([128, 1], F32)
        negB = AR.alloc([128, 1], F32)
        sm = AR.alloc([128, 8], F32)
        wslots = [AR.alloc([128, 4096], BF16) for _ in range(NSLOT)]
        s1acc = AR.alloc([128, 512], F32)
        s2acc = AR.alloc([128, 512], F32)
        lnr = AR.alloc([128, 512], F32)
        lnb = AR.alloc([128, 512], F32)
        tiles3 = [(0, t0) for t0 in (0, 512, 1024, 1536)] + [(1, 0), (1, 512)]
        y_keys = {}

        def conv_gen(tidx, acc, uwin, ysq, bankfn, pfx, immediate=False):
            seq, t0 = tiles3[tidx]
            cwt = cw[seq]
            cwk = "cw%d" % seq
            nu = len(uwin) // 2
            prev_stats = None
            for p in range(8):
                chains = []
                for n_, c in enumerate((2 * p, 2 * p + 1)):
                    ui = (p % nu) * 2 + n_
                    i_ = P.op("sp", lambda h, ui=ui, c=c: h.dma_start(out=uwin[ui][:, 0:542], in_=uT_s[seq][c * 128:(c + 1) * 128, t0:t0 + 542]),
                              r=[("uT", seq)], w=[(pfx + "uwin", ui)], dma=(pfx + "uwin", ui))
                    for k_ in list(P.dma_cnt):
                        if k_ == "uT_st" or (isinstance(k_, tuple) and k_[0] == "uT_st"):
                            i_.dmadeps[k_] = P.dma_cnt[k_]
                    chains.append((c, ui, 2 * n_))
                if prev_stats is not None:
                    prev_stats()
                    prev_stats = None
                for (c, ui, a0) in chains:
                    P.op("dve", lambda h, c=c, ui=ui, a0=a0: h.tensor_scalar(out=acc[a0], in0=uwin[ui][:, 0:512], scalar1=cwt[:, c, 0:1], scalar2=cb[:, c:c + 1], op0=ALU.mult, op1=ALU.add),
                         r=[(pfx + "uwin", ui), cwk, "cb"], w=[(pfx + "acc", a0)])
                yield None
                cur = 0
                for tap in range(1, TAPS):
                    nxt = 1 - cur
                    for (c, ui, a0) in chains:
                        P.op("dve", lambda h, tap=tap, cur=cur, nxt=nxt, c=c, ui=ui, a0=a0: h.scalar_tensor_tensor(
                            out=acc[a0 + nxt], in0=uwin[ui][:, tap:tap + 512], scalar=cwt[:, c, tap:tap + 1], in1=acc[a0 + cur], op0=ALU.mult, op1=ALU.add),
                            r=[(pfx + "uwin", ui), cwk, (pfx + "acc", a0 + cur)], w=[(pfx + "acc", a0 + nxt)])
                    cur = nxt
                    if tap < TAPS - 1:
                        yield None
                fins = []
                for n_, (c, ui, a0) in enumerate(chains):
                    fin = a0 + cur
                    fins.append((n_, c, fin))
                    P.op("act", lambda h, fin=fin, n_=n_: h.activation(out=ysq[n_], in_=acc[fin], func=AF.Square), r=[(pfx + "acc", fin)], w=[(pfx + "ysq", n_)])
                    P.op("sp", lambda h, fin=fin, c=c: h.dma_start(out=y_s[tidx, c * 128:(c + 1) * 128, :], in_=acc[fin]), r=[(pfx + "acc", fin)], w=[("y_s", tidx)], dma="yst")

                def stats(fins=fins):
                    for (n_, c, fin) in fins:
                        for (src, skey, dst, dkey) in ((acc[fin], (pfx + "acc", fin), s1acc, "s1acc"), (ysq[n_], (pfx + "ysq", n_), s2acc, "s2acc")):
                            b = bankfn()
                            mm(ps[b][:], onesf, src, True, True, ["onesf", skey], [("ps", b)])
                            if c == 0:
                                P.op("dve", lambda h, b=b, dst=dst: h.tensor_copy(out=dst, in_=ps[b][:]), r=[("ps", b)], w=[dkey])
                            else:
                                P.op("dve", lambda h, b=b, dst=dst: h.tensor_tensor(out=dst, in0=ps[b][:], in1=dst, op=ALU.add), r=[("ps", b), dkey], w=[dkey])
                if immediate:
                    stats()
                    yield "P"
                else:
                    prev_stats = stats
                    yield "B"
            if prev_stats is not None:
                prev_stats()
            yield "B"

        def drive(gen, n=8):
            if gen is None:
                return
            for _ in range(n):
                try:
                    if next(gen) == "B":
                        break
                except StopIteration:
                    break

        def run_pairs(gen, n):
            seen = 0
            while seen < n:
                try:
                    if next(gen) == "P":
                        seen += 1
                except StopIteration:
                    break

        def drain(gen):
            if gen is None:
                return
            for _ in gen:
                pass

        def ln_finalize(tmp0, tmp1, k0, k1):
            P.op("dve", lambda h: h.tensor_single_scalar(out=s1acc, in_=s1acc, scalar=1.0 / CW, op=ALU.mult), r=["s1acc"], w=["s1acc"])
            P.op("dve", lambda h: h.tensor_tensor(out=tmp0, in0=s1acc, in1=s1acc, op=ALU.mult), r=["s1acc"], w=[k0])
            P.op("dve", lambda h: h.scalar_tensor_tensor(out=tmp1, in0=s2acc, scalar=1.0 / CW, in1=tmp0, op0=ALU.mult, op1=ALU.subtract),
                 r=["s2acc", k0], w=[k1])
            P.op("act", lambda h: h.activation(out=tmp0, in_=tmp1, func=AF.Sqrt, bias=epsb), r=[k1, "epsb"], w=[k0])
            P.op("dve", lambda h: h.reciprocal(out=lnr, in_=tmp0), r=[k0], w=["lnr"])
            P.op("dve", lambda h: h.scalar_tensor_tensor(out=lnb, in0=s1acc, scalar=-1.0, in1=lnr, op0=ALU.mult, op1=ALU.mult), r=["s1acc", "lnr"], w=["lnb"])

        def ld(dst, src, key):
            P.op("sp", lambda h: h.dma_start(out=dst, in_=src), w=[key], dma="const")

        ld(gmix, gmix_d, "gmix"); ld(gffn, gffn_d, "gffn"); ld(gq, gq_d, "gq"); ld(gk, gk_d, "gk")
        ld(cw[0], cw_d[0], "cw0"); ld(cw[1], cw_d[1], "cw1"); ld(cb, cb_d, "cb"); ld(lg, lg_d, "lg"); ld(lb, lb_d, "lb")
        P.op("dve", lambda h: h.memset(identf, 0.0), w=["identf"])
        P.op("dve", lambda h: h.memset(onesf, 1.0), w=["onesf"])
        P.op("dve", lambda h: h.memset(epsb, EPS), w=["epsb"])
        P.op("pool", lambda h: h.affine_select(out=identf, in_=onesf, pattern=[[-1, 128]], compare_op=ALU.is_equal,
                                               fill=0.0, base=0, channel_multiplier=1), r=["onesf"], w=["identf"])
        P.op("dve", lambda h: h.tensor_copy(out=ident, in_=identf), r=["identf"], w=["ident"])
        P.op("dve", lambda h: h.tensor_copy(out=onesb, in_=onesf), r=["onesf"], w=["onesb"])
        P.op("dve", lambda h: h.tensor_reduce(out=sm[:, 0:1], in_=gq, axis=AX.X, op=ALU.max, apply_absolute_value=True), r=["gq"], w=["sm0"])
        P.op("dve", lambda h: h.tensor_reduce(out=sm[:, 1:2], in_=gk, axis=AX.X, op=ALU.max, apply_absolute_value=True), r=["gk"], w=["sm1"])
        P.op("dve", lambda h: h.tensor_tensor(out=sm[:, 2:3], in0=sm[:, 0:1], in1=sm[:, 1:2], op=ALU.mult), r=["sm0", "sm1"], w=["sm2"])
        P.op("dve", lambda h: h.tensor_single_scalar(out=negB, in_=sm[:, 2:3], scalar=-(128.0 ** 0.5), op=ALU.mult), r=["sm2"], w=["negB"])
        zt = AR.alloc([128, 16, 15], F32)
        P.op("dve", lambda h: h.memset(zt, 0.0), w=["zt"])
        for s in range(2):
            P.op("sp", lambda h, s=s: h.dma_start(out=uT_s[s][:, 0:15].rearrange("(c p) z -> p c z", p=128), in_=zt), r=["zt"], w=[("uT", s)], dma="uT_st")
        P.op("sp", lambda h: h.dma_start(out=uT_s[0][:, 15 + 2048:15 + 2048 + 15].rearrange("(c p) z -> p c z", p=128), in_=zt), r=["zt"], w=[("uT", 0)], dma="uT_st")

        deferred = []

        def wload(dram_ap, nk, ncols):
            i = st["slot"] % NSLOT
            st["slot"] += 1
            view = wslots[i][:, 0:nk * ncols].rearrange("p (k n) -> p k n", n=ncols)
            P.op("pool", lambda h: h.dma_start(out=view, in_=dram_ap), w=[("w", i)], dma=("w", i))
            return view, ("w", i)

        def mm(out, lhsT, rhs, start, stop, r, w):
            P.op("pe", lambda h: h.matmul(out, lhsT=lhsT, rhs=rhs, start=start, stop=stop), r=r, w=w)

        def tm_group(wblocks, lhs, lhs_keys, epi, nts=4, epi_all=None):
            banks = [next_bank() for _ in range(nts)]
            nkt = sum(nk for _, nk in wblocks)
            kg0 = 0
            for dap, nk in wblocks:
                view, wkey = wload(dap, nk, 512)
                for ts in range(nts):
                    for k in range(nk):
                        kg = kg0 + k
                        mm(ps[banks[ts]][:], lhs(kg, ts), view[:, k, :], kg == 0, kg == nkt - 1,
                           [wkey] + lhs_keys, [("ps", banks[ts])])
                kg0 += nk
            while deferred:
                deferred.pop(0)()
            if epi_all is not None:
                epi_all(banks)
            else:
                for ts in range(nts):
                    epi(ts, banks[ts])

        def fm_pair(wA, wB, rhs, rhs_keys, NT, epi):
            bA = [next_bank(), next_bank()]
            bB = [next_bank(), next_bank()]
            for kb2 in range(2):
                for blocks, banks in ((wA, bA), (wB, bB)):
                    view, wkey = wload(blocks[kb2], 16, 256)
                    for j in range(2):
                        for k in range(16):
                            kg = kb2 * 16 + k
                            mm(ps[banks[j]][:, 0:NT], view[:, k, j * 128:(j + 1) * 128], rhs(kg), kg == 0, kg == 31,
                               [wkey] + rhs_keys, [("ps", banks[j])])
            while deferred:
                deferred.pop(0)()
            for j in range(2):
                epi(j, bA[j], bB[j])

        def transposes_to(src_fn, src_keys, dst, dst_key, ts, gcol):
            for g in range(4):
                b = next_bank()
                for k in range(8):
                    kc = g * 8 + k
                    P.op("pe", lambda h, b=b, k=k, kc=kc: h.transpose(psb[b][:, k * 128:(k + 1) * 128], src_fn(kc), ident),
                         r=src_keys + ["ident"], w=[("ps", b)])
                P.op("dve", lambda h, b=b, g=g: h.tensor_tensor(
                    out=dst[:, g * 8:(g + 1) * 8, ts * 128:(ts + 1) * 128],
                    in0=psb[b].rearrange("p (k t) -> p k t", t=128),
                    in1=bc(gcol[:, g * 8:(g + 1) * 8].unsqueeze(2), [128, 8, 128]), op=ALU.mult),
                    r=[("ps", b), "gmix", "gffn"], w=[dst_key])

        base_mark = AR.mark()

        hnT = [AR.alloc([128, 32, 512], BF16) for _ in range(2)]
        xst = [AR.alloc([128, D], F32)] * 2
        hn_tm = [AR.alloc([128, D], BF16)] * 2
        cos_t = AR.alloc([128, 4, 64], F32)
        sin_t = AR.alloc([128, 4, 64], F32)
        rtab = [[AR.alloc([128, 4, 64], F32) for _ in range(4)] for _ in range(2)]
        sqt = [AR.alloc([128, 512], F32) for _ in range(4)]
        qn = [AR.alloc([128, 4, 128], F32) for _ in range(4)]
        rt = [AR.alloc([128, 4, 64], F32) for _ in range(8)]
        qo = [AR.alloc([128, 4, 128], BF16) for _ in range(4)]
        ssq = [AR.alloc([128, 8], F32) for _ in range(4)]
        qstage = [AR.alloc([128, 4, 512], BF16)] * 2
        vst = [AR.alloc([128, 512], BF16) for _ in range(2)]
        sg = [AR.alloc([128, 512], F32)] * 2
        ust = [AR.alloc([128, 512], F32) for _ in range(2)]
        acc1 = [AR.alloc([128, 512], F32) for _ in range(4)]
        uwin1 = [AR.alloc([128, 544], F32) for _ in range(2)]
        ysq1 = [AR.alloc([128, 512], F32) for _ in range(2)]
        print("phase1 arena", AR.off)
        rst1 = [AR.alloc([128, 2], F32) for _ in range(2)]
        tiles1 = [dict(seq=0, t0=t0, q=True, conv=512) for t0 in (0, 512, 1024, 1536)]
        tiles1 += [dict(seq=1, t0=0, q=True, conv=512), dict(seq=1, t0=512, q=True, conv=512),
                   dict(seq=1, t0=1024, q=False, conv=128), dict(seq=1, t0=1536, q=False, conv=0)]
        cnt1 = {"x": 0, "u": 0, "v": 0, "qk": 0}

        def prep_tables(ti):
            tl = tiles1[ti]
            blk0 = tl["t0"] // 128
            sq_ = tl["seq"]
            P.op("sp", lambda h: h.dma_start(out=cos_t, in_=cos_d[sq_][:, blk0:blk0 + 4, :]), w=["cos_t"], dma="rope_ld")
            P.op("sp", lambda h: h.dma_start(out=sin_t, in_=sin_d[sq_][:, blk0:blk0 + 4, :]), w=["sin_t"], dma="rope_ld")
            for qk_i, (gv, gkey) in enumerate(((gq, "gq"), (gk, "gk"))):
                g1 = bc(gv[:, 0:64].unsqueeze(1), [128, 4, 64])
                g2 = bc(gv[:, 64:128].unsqueeze(1), [128, 4, 64])
                for n_, (tab, gg) in enumerate(((cos_t, g1), (sin_t, g2), (sin_t, g1), (cos_t, g2))):
                    P.op("dve", lambda h, tab=tab, gg=gg, dst=rtab[qk_i][n_]: h.tensor_tensor(out=dst, in0=tab, in1=gg, op=ALU.mult),
                         r=["cos_t", "sin_t", gkey], w=[("rtab", qk_i)])

        def prep_norm(ti, ts):
            tl = tiles1[ti]
            xb_ = cnt1["x"] % 2
            cnt1["x"] += 1
            r0 = tl["t0"] + ts * 128
            P.op("sp", lambda h, xb_=xb_, r0=r0, s=tl["seq"]: h.dma_start(out=xst[xb_], in_=xs_d[s][r0:r0 + 128, :]),
                 w=[("xst", 0)], dma=("xst", 0))
            P.op("dve", lambda h, xb_=xb_: h.memset(rst1[xb_][:, 0:1], 0.0), w=[("rst1a", xb_)])
            P.op("act", lambda h, xb_=xb_: h.activation(out=hn_tm[xb_], in_=xst[xb_], func=AF.Square, accum_out=rst1[xb_][:, 0:1]),
                 r=[("xst", 0)], w=[("hn_tm", 0), ("rst1a", xb_)])
            P.op("act", lambda h, xb_=xb_: h.activation(out=rst1[xb_][:, 1:2], in_=rst1[xb_][:, 0:1], func=AF.Sqrt, scale=1.0 / D, bias=epsb),
                 r=[("rst1a", xb_), "epsb"], w=[("rst1b", xb_)])
            P.op("dve", lambda h, xb_=xb_: h.reciprocal(out=rst1[xb_][:, 0:1], in_=rst1[xb_][:, 1:2]),
                 r=[("rst1b", xb_)], w=[("rst1a", xb_)])
            P.op("act", lambda h, xb_=xb_: h.activation(out=hn_tm[xb_], in_=xst[xb_], func=AF.Copy, scale=rst1[xb_][:, 0:1]),
                 r=[("xst", 0), ("rst1a", xb_)], w=[("hn_tm", 0)])

        def prep_tr(ti, ts):
            hb = ti % 2
            transposes_to(lambda kc: hn_tm[0][:, kc * 128:(kc + 1) * 128], [("hn_tm", 0)],
                          hnT[hb], ("hnT", hb), ts, gmix)

        def prep1(ti):
            prep_tables(ti)
            for ts in range(4):
                prep_norm(ti, ts)
                prep_tr(ti, ts)

        def qk_epi(tl, hb, cgi, is_q):
            qk_i = 0 if is_q else 1
            dst = (qT_s if is_q else kT_s)[tl["seq"]]
            h0 = (cgi % 4) * 4
            sb_ = cnt1["qk"] % 2
            cnt1["qk"] += 1
            stage = qstage[sb_]
            skey = ("qstage", 0)
            seq = tl["seq"]
            C1, S2, S1, C2 = rtab[qk_i]
            tk = ("rtab", qk_i)

            def epi_all(banks):
                R = range(4)
                for ts in R:
                    if ts < 2:
                        P.op("act", lambda h, ts=ts: h.activation(out=qn[ts].rearrange("p a d -> p (a d)"), in_=ps[banks[ts]][:], func=AF.Copy),
                             r=[("ps", banks[ts])], w=[("qn", ts)])
                    else:
                        P.op("dve", lambda h, ts=ts: h.tensor_copy(out=qn[ts].rearrange("p a d -> p (a d)"), in_=ps[banks[ts]][:]),
                             r=[("ps", banks[ts])], w=[("qn", ts)])
                for ts in R:
                    P.op("act", lambda h, ts=ts: h.activation(out=sqt[ts], in_=qn[ts].rearrange("p a d -> p (a d)"), func=AF.Square), r=[("qn", ts)], w=[("sqt", ts)])
                for ts in R:
                    P.op("dve", lambda h, ts=ts: h.tensor_reduce(out=ssq[ts][:, 0:4], in_=sqt[ts].rearrange("p (a d) -> p a d", d=128), axis=AX.X, op=ALU.add),
                         r=[("sqt", ts)], w=[("ssq", ts)])
                for ts in R:
                    P.op("act", lambda h, ts=ts: h.activation(out=ssq[ts][:, 4:8], in_=ssq[ts][:, 0:4], func=AF.Sqrt, scale=1.0 / HD, bias=epsb),
                         r=[("ssq", ts), "epsb"], w=[("ssqb", ts)])
                for ts in R:
                    P.op("dve", lambda h, ts=ts: h.reciprocal(out=ssq[ts][:, 0:4], in_=ssq[ts][:, 4:8]), r=[("ssqb", ts)], w=[("ssq", ts)])
                for ts in R:
                    P.op("dve", lambda h, ts=ts: h.tensor_tensor(out=qn[ts], in0=qn[ts],
                                                                 in1=bc(ssq[ts][:, 0:4].unsqueeze(2), [128, 4, 128]), op=ALU.mult),
                         r=[("qn", ts), ("ssq", ts)], w=[("qn", ts)])
                for (ta, tb_, half, op_) in ((C1, S2, 0, ALU.subtract), (S1, C2, 1, ALU.add)):
                    for ts in R:
                        P.op("dve", lambda h, ts=ts, ta=ta: h.tensor_tensor(out=rt[2 * ts], in0=qn[ts][:, :, 0:64], in1=bc(ta[:, ts, :].unsqueeze(1), [128, 4, 64]), op=ALU.mult),
                             r=[("qn", ts), tk], w=[("rt", 2 * ts)])
                    for ts in R:
                        P.op("dve", lambda h, ts=ts, tb_=tb_: h.tensor_tensor(out=rt[2 * ts + 1], in0=qn[ts][:, :, 64:128], in1=bc(tb_[:, ts, :].unsqueeze(1), [128, 4, 64]), op=ALU.mult),
                             r=[("qn", ts), tk], w=[("rt", 2 * ts + 1)])
                    for ts in R:
                        P.op("dve", lambda h, ts=ts, half=half, op_=op_: h.tensor_tensor(out=qo[ts][:, :, half * 64:(half + 1) * 64], in0=rt[2 * ts], in1=rt[2 * ts + 1], op=op_),
                             r=[("rt", 2 * ts), ("rt", 2 * ts + 1)], w=[("qo", ts)])

                def late():
                    tbs = [next_bank(), next_bank()]
                    for ts in R:
                        tb = tbs[ts // 2]
                        o0 = (ts % 2) * 512
                        for a_ in range(4):
                            P.op("pe", lambda h, a_=a_, ts=ts, tb=tb, o0=o0: h.transpose(psb[tb][:, o0 + a_ * 128:o0 + (a_ + 1) * 128], qo[ts][:, a_, :], ident),
                                 r=[("qo", ts), "ident"], w=[("ps", tb)])
                    for ts in R:
                        tb = tbs[ts // 2]
                        o0 = (ts % 2) * 512
                        eng_ = "act" if ts % 2 == 0 else "dve"
                        if eng_ == "act":
                            P.op("act", lambda h, ts=ts, tb=tb, o0=o0: h.activation(out=stage[:, :, ts * 128:(ts + 1) * 128],
                                                                                    in_=psb[tb][:, o0:o0 + 512].rearrange("p (a t) -> p a t", t=128), func=AF.Copy),
                                 r=[("ps", tb)], w=[skey])
                        else:
                            P.op("dve", lambda h, ts=ts, tb=tb, o0=o0: h.tensor_copy(out=stage[:, :, ts * 128:(ts + 1) * 128],
                                                                                     in_=psb[tb][:, o0:o0 + 512].rearrange("p (a t) -> p a t", t=128)),
                                 r=[("ps", tb)], w=[skey])
                    t0 = tl["t0"]
                    P.op("sp", lambda h: h.dma_start(out=dst[h0:h0 + 4, :, t0:t0 + 512].rearrange("a d t -> d a t"), in_=stage),
                         r=[skey], w=[("qk_s", seq)], dma=("qk_st", sb_))
                deferred.append(late)
            return epi_all

        def v_epi(tl, cgi):
            seq = tl["seq"]

            def epi(ts, b):
                i3 = cnt1["v"] % 2
                cnt1["v"] += 1
                r0 = tl["t0"] + ts * 128
                c0 = (cgi - 8) * 512
                P.op("act", lambda h: h.activation(out=vst[i3], in_=ps[b][:], func=AF.Copy), r=[("ps", b)], w=[("vst", i3)])
                P.op("sp", lambda h: h.dma_start(out=v_s[seq][r0:r0 + 128, c0:c0 + 512], in_=vst[i3]), r=[("vst", i3)], w=[("v_s", seq)], dma=("v_st", i3))
            return epi

        def glu_epi(tl, gi, NT):
            seq = tl["seq"]

            def epi(j, bA_, bB_):
                i2 = uid() % 2
                i3 = cnt1["u"] % 2
                cnt1["u"] += 1
                c = gi * 2 + j
                c0 = 15 + tl["t0"]
                P.op("act", lambda h: h.activation(out=sg[i2][:, 0:NT], in_=ps[bB_][:, 0:NT], func=AF.Sigmoid), r=[("ps", bB_)], w=[("sg", 0)])
                P.op("dve", lambda h: h.tensor_tensor(out=ust[i3][:, 0:NT], in0=ps[bA_][:, 0:NT], in1=sg[i2][:, 0:NT], op=ALU.mult),
                     r=[("ps", bA_), ("sg", 0)], w=[("ust", i3)])
                P.op("sp", lambda h: h.dma_start(out=uT_s[seq][c * 128:(c + 1) * 128, c0:c0 + NT], in_=ust[i3][:, 0:NT]),
                     r=[("ust", i3)], w=[("uT", seq)], dma=("uT_st", i3))
            return epi

        prep1(0)
        gen0 = conv_gen(0, acc1, uwin1, ysq1, next_bank, "p1")
        for ti, tl in enumerate(tiles1):
            hb = ti % 2
            groups = []
            for cgi in range(12):
                if cgi < 4 and not tl["q"]:
                    continue
                groups.append(("tm", cgi))
            if tl["conv"]:
                for gi in range(8):
                    groups.append(("fm", gi))
            lhs = lambda kg, ts, hb=hb: hnT[hb][:, kg, ts * 128:(ts + 1) * 128]
            for gidx, (kind, gi) in enumerate(groups):
                G_ = len(groups)
                if ti + 1 < len(tiles1):
                    k_ = gidx - (G_ - 5)
                    if k_ == 0:
                        prep_tables(ti + 1)
                    if 1 <= k_ <= 4:
                        prep_tr(ti + 1, k_ - 1)
                    if 0 <= k_ <= 3:
                        prep_norm(ti + 1, k_)
                if ti >= 2 and not (kind == "tm" and gi < 8):
                    drive(gen0, 8)
                if kind == "tm":
                    wb = [(w_in_v[:, kb * 8:(kb + 1) * 8, gi * 512:(gi + 1) * 512], 8) for kb in range(4)]
                    if gi < 8:
                        tm_group(wb, lhs, [("hnT", hb)], None, epi_all=qk_epi(tl, hb, gi, gi < 4))
                    else:
                        tm_group(wb, lhs, [("hnT", hb)], v_epi(tl, gi))
                else:
                    NT = tl["conv"]
                    ca = 3 * AW + gi * 256
                    cg_ = 3 * AW + CW + gi * 256
                    wA = [w_in_v[:, kb * 16:(kb + 1) * 16, ca:ca + 256] for kb in range(2)]
                    wB = [w_in_v[:, kb * 16:(kb + 1) * 16, cg_:cg_ + 256] for kb in range(2)]
                    fm_pair(wA, wB, lambda kg, hb=hb, NT=NT: hnT[hb][:, kg, 0:NT], [("hnT", hb)], NT, glu_epi(tl, gi, NT))
            while deferred:
                deferred.pop(0)()
        drain(gen0)
        ln_finalize(acc1[0], acc1[1], ("p1acc", 0), ("p1acc", 1))

        P.barrier()
        AR.reset(base_mark)

        maskM = AR.alloc([128, MASKW], BF16)
        P.op("pool", lambda h: h.dma_start(out=maskM, in_=mask_d), w=["maskM"], dma="maskld")
        kT = [AR.alloc([128, S], BF16) for _ in range(3)]
        vh = [AR.alloc([128, 16, 128], BF16) for _ in range(3)]
        qT = [AR.alloc([128, S], BF16) for _ in range(3)]
        NPT = 8
        LOOK = 6
        pt = [AR.alloc([128, 512], BF16) for _ in range(NPT)]
        rl = [AR.alloc([128, 512], F32) for _ in range(2)]
        ast = [AR.alloc([128, 512], BF16) for _ in range(2)]
        SB = [0, 1, 2]
        OB = [3, 4]
        LB = [5, 6]
        scale = float(HD) ** -0.5
        heads = [(seq, hh) for seq in range(2) for hh in range(NH)]

        def load_head(n):
            seq, hh = heads[n]
            i = n % 3
            tq = TQ[seq]
            P.op("sp", lambda h: h.dma_start(out=kT[i], in_=kT_s[seq][hh]), r=[("qk_s", seq)], w=[("kT", i)], dma=("kT", i))
            P.op("sp", lambda h: h.dma_start(out=vh[i], in_=v_s[seq][:, hh * 128:(hh + 1) * 128].rearrange("(kb p) d -> p kb d", p=128)),
                 r=[("v_s", seq)], w=[("vh", i)], dma=("vh", i))
            P.op("sp", lambda h: h.dma_start(out=qT[i][:, 0:tq], in_=qT_s[seq][hh]), r=[("qk_s", seq)], w=[("qT", i)], dma=("qT", i))

        steps = []
        qbc = 0
        for n, (seq, hh) in enumerate(heads):
            for qb in range(TQ[seq] // 512):
                q0 = qb * 512
                kbs = [kb for kb in range(16) if kb * 128 + 127 >= q0 - 1024 and kb * 128 <= q0 + 511 + 1024]
                oi = qbc % 2
                qbc += 1
                for kb in kbs:
                    steps.append(dict(n=n, i=n % 3, seq=seq, hh=hh, q0=q0, kb=kb, first=(kb == kbs[0]), last=(kb == kbs[-1]),
                                      oi=oi, head_first=(qb == 0 and kb == kbs[0])))
        ctr = {"s": 0, "pt": 0}

        def emit_qk(sp_):
            i, q0, kb = sp_["i"], sp_["q0"], sp_["kb"]
            sbk = SB[ctr["s"] % 3]
            ctr["s"] += 1
            pi = ctr["pt"] % NPT
            ctr["pt"] += 1
            mm(ps[sbk][:], kT[i][:, kb * 128:(kb + 1) * 128], qT[i][:, q0:q0 + 512], True, True,
               [("kT", i), ("qT", i)], [("ps", sbk)])
            P.op("act", lambda h: h.activation(out=pt[pi], in_=ps[sbk][:], func=AF.Exp, scale=scale, bias=negB),
                 r=[("ps", sbk), "negB"], w=[("pt", pi)])
            off = U0 - (kb * 128 - q0)
            P.op("dve", lambda h: h.tensor_tensor(out=pt[pi], in0=pt[pi], in1=maskM[:, off:off + 512], op=ALU.mult),
                 r=[("pt", pi), "maskM"], w=[("pt", pi)])
            return pi

        def emit_pv(sp_, pi):
            i, kb, oi = sp_["i"], sp_["kb"], sp_["oi"]
            ob, lbk = OB[oi], LB[oi]
            mm(ps[ob][:], vh[i][:, kb, :], pt[pi], sp_["first"], sp_["last"], [("vh", i), ("pt", pi)], [("ps", ob)])
            mm(ps[lbk][:], onesb, pt[pi], sp_["first"], sp_["last"], ["onesb", ("pt", pi)], [("ps", lbk)])
            if sp_["last"]:
                seq, hh, q0 = sp_["seq"], sp_["hh"], sp_["q0"]
                P.op("dve", lambda h: h.reciprocal(out=rl[oi], in_=ps[lbk][:]), r=[("ps", lbk)], w=[("rl", oi)])
                P.op("dve", lambda h: h.tensor_tensor(out=ast[oi], in0=ps[ob][:], in1=rl[oi], op=ALU.mult),
                     r=[("ps", ob), ("rl", oi)], w=[("ast", oi)])
                P.op("sp", lambda h: h.dma_start(out=attnT_s[seq][hh * 128:(hh + 1) * 128, q0:q0 + 512], in_=ast[oi]),
                     r=[("ast", oi)], w=[("attn_s", seq)], dma=("attn_st", oi))

        load_head(0)
        load_head(1)
        pend = []
        for sp_ in steps:
            if sp_["head_first"] and sp_["n"] >= 1 and sp_["n"] + 1 < len(heads):
                load_head(sp_["n"] + 1)
            pend.append((sp_, emit_qk(sp_)))
            if len(pend) > LOOK:
                a_, b_ = pend.pop(0)
                emit_pv(a_, b_)
        while pend:
            a_, b_ = pend.pop(0)
            emit_pv(a_, b_)

        P.barrier()
        AR.reset(base_mark)
        st["bank"] = 0

        R1 = AR.alloc([128, NFF, 512], BF16)
        r1_off = AR.off - NFF * 512 * 2
        hbuf = arena_t[:, r1_off // 2:(r1_off + 4 * D * 4) // 2].bitcast(F32).rearrange("p (a b) -> p a b", b=D)
        tail = r1_off + 4 * D * 4
        hf_tm = [arena_t[:, (tail + i * D * 2) // 2:(tail + (i + 1) * D * 2) // 2] for i in range(2)]
        assert tail + 2 * D * 2 <= r1_off + NFF * 512 * 2
        X0 = AR.alloc([128, 32, 512], BF16)
        inst = [AR.alloc([128, 512], F32) for _ in range(4)]
        ost = [AR.alloc([128, 512], F32) for _ in range(3)]
        uwin = [AR.alloc([128, 544], F32) for _ in range(4)]
        sgt = [AR.alloc([128, 512], F32) for _ in range(2)]
        acc = [AR.alloc([128, 512], F32) for _ in range(4)]
        ysq = [AR.alloc([128, 512], F32) for _ in range(2)]
        sqj = AR.alloc([128, 512], BF16)
        X0ALL = ["X0"] + [("X0c", c) for c in range(16)]
        ssp = AR.alloc([128, 4, 8], F32)
        rs4 = AR.alloc([128, 8], F32)

        print("phase3 arena", AR.off)
        c3 = {"in": 0, "o": 0, "uw": 0}

        S1B, S2B = 6, 7

        def next_bank6():
            b = st["bank"]
            st["bank"] = (b + 1) % 6
            return b

        def conv_hooks_r3(tidx):
            seq, t0 = tiles3[tidx]
            cwt = cw[seq]
            cwk = "cw%d" % seq

            def conv_pair(ca, cb_):
                par = c3["uw"] % 2
                c3["uw"] += 1
                chains = []
                for n_, c in enumerate((ca, cb_)):
                    ui = 2 * par + n_
                    P.op("sp", lambda h, ui=ui, c=c: h.dma_start(out=uwin[ui][:, 0:542], in_=uT_s[seq][c * 128:(c + 1) * 128, t0:t0 + 542]),
                         r=[("uT", seq)], w=[("p3uwin", ui)], dma=("p3uwin", ui))
                    chains.append((c, ui, 2 * n_))
                for (c, ui, a0) in chains:
                    P.op("dve", lambda h, c=c, ui=ui, a0=a0: h.tensor_scalar(out=acc[a0], in0=uwin[ui][:, 0:512], scalar1=cwt[:, c, 0:1], scalar2=cb[:, c:c + 1], op0=ALU.mult, op1=ALU.add),
                         r=[("p3uwin", ui), cwk, "cb"], w=[("p3acc", a0)])
                cur = 0
                for tap in range(1, TAPS):
                    nxt = 1 - cur
                    for (c, ui, a0) in chains:
                        P.op("dve", lambda h, tap=tap, cur=cur, nxt=nxt, c=c, ui=ui, a0=a0: h.scalar_tensor_tensor(
                            out=acc[a0 + nxt], in0=uwin[ui][:, tap:tap + 512], scalar=cwt[:, c, tap:tap + 1], in1=acc[a0 + cur], op0=ALU.mult, op1=ALU.add),
                            r=[("p3uwin", ui), cwk, ("p3acc", a0 + cur)], w=[("p3acc", a0 + nxt)])
                    cur = nxt
                for n_, (c, ui, a0) in enumerate(chains):
                    fin = a0 + cur
                    P.op("act", lambda h, fin=fin, n_=n_: h.activation(out=ysq[n_], in_=acc[fin], func=AF.Square), r=[("p3acc", fin)], w=[("p3ysq", n_)])
                    P.op("act", lambda h, fin=fin, c=c: h.activation(out=X0[:, 16 + c, :], in_=acc[fin], func=AF.Copy), r=[("p3acc", fin)], w=[("X0c", c)])
                for n_, (c, ui, a0) in enumerate(chains):
                    fin = a0 + cur
                    mm(ps[S1B][:], onesf, acc[fin], c == 0, c == 15, ["onesf", ("p3acc", fin)], [("ps", S1B)])
                    mm(ps[S2B][:], onesf, ysq[n_], c == 0, c == 15, ["onesf", ("p3ysq", n_)], [("ps", S2B)])

            def finalize():
                P.op("dve", lambda h: h.tensor_single_scalar(out=s1acc, in_=ps[S1B][:], scalar=1.0 / CW, op=ALU.mult), r=[("ps", S1B)], w=["s1acc"])
                P.op("dve", lambda h: h.tensor_tensor(out=acc[0], in0=s1acc, in1=s1acc, op=ALU.mult), r=["s1acc"], w=[("p3acc", 0)])
                P.op("dve", lambda h: h.scalar_tensor_tensor(out=acc[1], in0=ps[S2B][:], scalar=1.0 / CW, in1=acc[0], op0=ALU.mult, op1=ALU.subtract),
                     r=[("ps", S2B), ("p3acc", 0)], w=[("p3acc", 1)])
                P.op("act", lambda h: h.activation(out=acc[0], in_=acc[1], func=AF.Sqrt, bias=epsb), r=[("p3acc", 1), "epsb"], w=[("p3acc", 0)])
                P.op("dve", lambda h: h.reciprocal(out=lnr, in_=acc[0]), r=[("p3acc", 0)], w=["lnr"])
                P.op("dve", lambda h: h.scalar_tensor_tensor(out=lnb, in0=s1acc, scalar=-1.0, in1=lnr, op0=ALU.mult, op1=ALU.mult), r=["s1acc", "lnr"], w=["lnb"])

            def norm4(c0):
                cs = range(c0, c0 + 4)
                for c in cs:
                    i4 = c % 4
                    P.op("dve", lambda h, c=c, i4=i4: h.tensor_tensor(out=acc[i4], in0=X0[:, 16 + c, :], in1=lnr, op=ALU.mult), r=[("X0c", c), "lnr"], w=[("p3acc", i4)])
                for c in cs:
                    i4 = c % 4
                    P.op("dve", lambda h, c=c, i4=i4: h.tensor_tensor(out=acc[i4], in0=acc[i4], in1=lnb, op=ALU.add), r=[("p3acc", i4), "lnb"], w=[("p3acc", i4)])
                for c in cs:
                    i4 = c % 4
                    P.op("act", lambda h, c=c, i4=i4: h.activation(out=X0[:, 16 + c, :], in_=acc[i4], func=AF.Silu, scale=lg[:, c:c + 1], bias=lb[:, c:c + 1]),
                         r=[("p3acc", i4), "lg", "lb"], w=[("X0c", c)])

            return [(lambda: (attn_load(tidx), conv_pair(0, 1), conv_pair(2, 3))),
                    (lambda: conv_pair(4, 5)),
                    (lambda: conv_pair(6, 7)),
                    (lambda: conv_pair(8, 9)),
                    (lambda: conv_pair(10, 11)),
                    (lambda: conv_pair(12, 13)),
                    (lambda: conv_pair(14, 15)),
                    (lambda: (finalize(), norm4(0), norm4(4), norm4(8), norm4(12)))]

        def normalize4(tidx, c0):
            cs = range(c0, c0 + 4)
            for c in cs:
                i4 = c % 4
                P.op("sp", lambda h, c=c, i4=i4: h.dma_start(out=acc[i4], in_=y_s[tidx, c * 128:(c + 1) * 128, :]), r=[("y_s", tidx)], w=[("p3acc", i4)], dma=("yld", i4))
            for c in cs:
                i4 = c % 4
                P.op("dve", lambda h, c=c, i4=i4: h.tensor_tensor(out=acc[i4], in0=acc[i4], in1=lnr, op=ALU.mult), r=[("p3acc", i4), "lnr"], w=[("p3acc", i4)])
            for c in cs:
                i4 = c % 4
                P.op("dve", lambda h, c=c, i4=i4: h.tensor_tensor(out=acc[i4], in0=acc[i4], in1=lnb, op=ALU.add), r=[("p3acc", i4), "lnb"], w=[("p3acc", i4)])
            for c in cs:
                i4 = c % 4
                P.op("act", lambda h, c=c, i4=i4: h.activation(out=X0[:, 16 + c, :], in_=acc[i4], func=AF.Silu, scale=lg[:, c:c + 1], bias=lb[:, c:c + 1]),
                     r=[("p3acc", i4), "lg", "lb"], w=[("X0c", c)])

        def attn_load(tidx):
            seq, t0 = tiles3[tidx]
            P.op("sp", lambda h: h.dma_start(out=X0[:, 0:16, :], in_=attnT_s[seq][:, t0:t0 + 512].rearrange("(c p) t -> p c t", p=128)),
                 r=[("attn_s", seq)], w=["X0"], dma="X0ld")

        def outproj(ti, seq, t0):
            def epi_for(cg):
                def epi(ts, b):
                    ii = c3["in"] % 4
                    c3["in"] += 1
                    r0 = t0 + ts * 128
                    P.op("sp", lambda h: h.dma_start(out=inst[ii], in_=xs_d[seq][r0:r0 + 128, cg * 512:(cg + 1) * 512]), w=[("inst", ii)], dma=("inst", ii))
                    P.op("dve", lambda h: h.tensor_tensor(out=hbuf[:, ts, cg * 512:(cg + 1) * 512], in0=ps[b][:], in1=inst[ii], op=ALU.add),
                         r=[("ps", b), ("inst", ii)], w=["R1"])
                    P.op("act", lambda h: h.activation(out=sqj, in_=hbuf[:, ts, cg * 512:(cg + 1) * 512], func=AF.Square, accum_out=ssp[:, ts, cg:cg + 1]),
                         r=["R1"], w=["sqj", "ssp"])
                return epi
            P.op("dve", lambda h: h.memset(ssp, 0.0), w=["ssp"])
            for cg in range(8):
                wb = [(w_out_v[:, kb * 8:(kb + 1) * 8, cg * 512:(cg + 1) * 512], 8) for kb in range(4)]
                tm_group6(wb, lambda kg, ts: X0[:, kg, ts * 128:(ts + 1) * 128], X0ALL, epi_for(cg))
            P.op("dve", lambda h: h.tensor_reduce(out=rs4[:, 0:4], in_=ssp, axis=AX.X, op=ALU.add), r=["ssp"], w=["rs4a"])
            P.op("act", lambda h: h.activation(out=rs4[:, 4:8], in_=rs4[:, 0:4], func=AF.Sqrt, scale=1.0 / D, bias=epsb), r=["rs4a", "epsb"], w=["rs4b"])
            P.op("dve", lambda h: h.reciprocal(out=rs4[:, 0:4], in_=rs4[:, 4:8]), r=["rs4b"], w=["rs4a"])
            for ts in range(4):
                fi = ts % 2
                P.op("sp", lambda h, ts=ts: h.dma_start(out=h_s[ti, ts * 128:(ts + 1) * 128, :], in_=hbuf[:, ts, :]), r=["R1"], w=["h_s"], dma="hs_st")
                P.op("act", lambda h, ts=ts, fi=fi: h.activation(out=hf_tm[fi], in_=hbuf[:, ts, :], func=AF.Copy, scale=rs4[:, ts:ts + 1]),
                     r=["R1", "rs4a"], w=[("hf_tm", fi)])
                transposes_to6(lambda kc, fi=fi: hf_tm[fi][:, kc * 128:(kc + 1) * 128], [("hf_tm", fi)], X0, X0ALL, ts, gffn)

        def tm_group6(wblocks, lhs, lhs_keys, epi, hook=None):
            banks = [next_bank6() for _ in range(4)]
            nkt = sum(nk for _, nk in wblocks)
            kg0 = 0
            for dap, nk in wblocks:
                view, wkey = wload(dap, nk, 512)
                for ts in range(4):
                    for k in range(nk):
                        kg = kg0 + k
                        mm(ps[banks[ts]][:], lhs(kg, ts), view[:, k, :], kg == 0, kg == nkt - 1,
                           [wkey] + lhs_keys, [("ps", banks[ts])])
                kg0 += nk
            if hook is not None:
                hook()
            for ts in range(4):
                epi(ts, banks[ts])

        def transposes_to6(src_fn, src_keys, dst, dst_key, ts, gcol):
            for g in range(4):
                b = next_bank6()
                for k in range(8):
                    kc = g * 8 + k
                    P.op("pe", lambda h, b=b, k=k, kc=kc: h.transpose(psb[b][:, k * 128:(k + 1) * 128], src_fn(kc), ident),
                         r=src_keys + ["ident"], w=[("ps", b)])
                P.op("dve", lambda h, b=b, g=g: h.tensor_tensor(
                    out=dst[:, g * 8:(g + 1) * 8, ts * 128:(ts + 1) * 128],
                    in0=psb[b].rearrange("p (k t) -> p k t", t=128),
                    in1=bc(gcol[:, g * 8:(g + 1) * 8].unsqueeze(2), [128, 8, 128]), op=ALU.mult),
                    r=[("ps", b), "gmix", "gffn"], w=list(dst_key))

        def gateup(gen=None):
            for gi in range(NFF // 2):
                if gi >= 1:
                    drive(gen, 8)
                c0 = gi * 256
                wA = [w_gate_v[:, kb * 16:(kb + 1) * 16, c0:c0 + 256] for kb in range(2)]
                wB = [w_up_v[:, kb * 16:(kb + 1) * 16, c0:c0 + 256] for kb in range(2)]
                bA = [next_bank6(), next_bank6()]
                bB = [next_bank6(), next_bank6()]
                for kb2 in range(2):
                    for blocks, banks in ((wA, bA), (wB, bB)):
                        view, wkey = wload(blocks[kb2], 16, 256)
                        for j in range(2):
                            for k in range(16):
                                kg = kb2 * 16 + k
                                mm(ps[banks[j]][:], view[:, k, j * 128:(j + 1) * 128], X0[:, kg, :], kg == 0, kg == 31,
                                   [wkey] + X0ALL, [("ps", banks[j])])
                for j in range(2):
                    i2 = uid() % 2
                    ffc = gi * 2 + j
                    P.op("act", lambda h, b=bA[j], i2=i2: h.activation(out=sgt[i2], in_=ps[b][:], func=AF.Silu), r=[("ps", bA[j])], w=[("sgt", i2)])
                    P.op("dve", lambda h, b=bB[j], i2=i2, ffc=ffc: h.tensor_tensor(out=R1[:, ffc, :], in0=ps[b][:], in1=sgt[i2], op=ALU.mult),
                         r=[("ps", bB[j]), ("sgt", i2)], w=["R1"])

        def down(ti, seq, t0, hooks):
            for cg in range(8):
                wb = []
                for fb in range(11):
                    nk = 8 if fb < 10 else 6
                    wb.append((w_down_v[:, fb * 8:fb * 8 + nk, cg * 512:(cg + 1) * 512], nk))

                def epi(ts, b, cg=cg):
                    ii = c3["in"] % 4
                    c3["in"] += 1
                    oi = c3["o"] % 3
                    c3["o"] += 1
                    r0 = t0 + ts * 128
                    P.op("sp", lambda h: h.dma_start(out=inst[ii], in_=h_s[ti, ts * 128:(ts + 1) * 128, cg * 512:(cg + 1) * 512]), r=["h_s"], w=[("inst", ii)], dma=("inst", ii))
                    P.op("dve", lambda h: h.tensor_tensor(out=ost[oi], in0=ps[b][:], in1=inst[ii], op=ALU.add), r=[("ps", b), ("inst", ii)], w=[("ost", oi)])
                    P.op("sp", lambda h: h.dma_start(out=ys_d[seq][r0:r0 + 128, cg * 512:(cg + 1) * 512], in_=ost[oi]), r=[("ost", oi)], dma=("out", oi))
                tm_group6(wb, lambda kg, ts: R1[:, kg, ts * 128:(ts + 1) * 128], ["R1"], epi, hook=(hooks[cg] if hooks else None))

        attn_load(0)
        for c0 in (0, 4, 8, 12):
            normalize4(0, c0)
        for ti, (seq, t0) in enumerate(tiles3):
            outproj(ti, seq, t0)
            hooks = None
            gen = None
            if ti + 1 < len(tiles3):
                hooks = conv_hooks_r3(ti + 1)
            gateup(gen)
            down(ti, seq, t0, hooks)

        fin = [("out", 0), ("out", 1), ("out", 2)]
        if DEBUG:
            fin = list(P.dma_cnt.keys())
        P.emit(nc, es, final_wait_keys=fin)
    return nc


def _mask_table():
    i = np.arange(128)[:, None]
    u = np.arange(MASKW)[None, :]
    dlt = np.abs(i - u + U0)
    c = (dlt <= 64).astype(np.float32)
    c += ((dlt % 4 == 0) & (dlt <= 256)).astype(np.float32)
    c += ((dlt % 16 == 0) & (dlt <= 1024)).astype(np.float32)
    return np.ascontiguousarray(c, dtype=np.float32)


def _rope_tables(rev):
    half = HD // 2
    freqs = (10000.0 ** (-np.arange(half, dtype=np.float32) * 2.0 / HD)).astype(np.float32)
    pos = np.arange(S, dtype=np.float32)
    if rev:
        pos = pos[::-1]
    ang = (pos[:, None] * freqs[None, :]).astype(np.float32)
    cos = np.cos(ang).astype(np.float32).reshape(16, 128, 64).transpose(1, 0, 2)
    sin = np.sin(ang).astype(np.float32).reshape(16, 128, 64).transpose(1, 0, 2)
    return np.ascontiguousarray(cos), np.ascontiguousarray(sin)


def _col(v, n):
    return np.ascontiguousarray(np.asarray(v, np.float32).reshape(n, 128).T)


_NC_CACHE = {}


def make_in_maps(x_prompt, x_sample, norm_mix_g, w_in, q_norm_g, k_norm_g, conv_w, conv_b, conv_ln_g, conv_ln_b,
                 w_out, norm_ffn_g, w_gate, w_up, w_down):
    f = lambda a: np.ascontiguousarray(np.asarray(a, dtype=np.float32))
    x_prompt = f(x_prompt); x_sample = f(x_sample)
    shared = dict(
        w_in=f(w_in[0]), w_out=f(w_out[0]), w_gate=f(w_gate[0]), w_up=f(w_up[0]), w_down=f(w_down[0]),
        gmix=_col(norm_mix_g[0], 32), gffn=_col(norm_ffn_g[0], 32),
        gq=np.ascontiguousarray(np.broadcast_to(f(q_norm_g[0])[None, :], (128, 128))),
        gk=np.ascontiguousarray(np.broadcast_to(f(k_norm_g[0])[None, :], (128, 128))),
        cb=_col(conv_b[0], 16), lg=_col(conv_ln_g[0], 16), lb=_col(conv_ln_b[0], 16),
        maskm=_mask_table(),
    )
    cwn = f(conv_w[0])
    cw_f = np.ascontiguousarray(cwn.reshape(TAPS, 16, 128).transpose(2, 1, 0))
    cw_r = np.ascontiguousarray(cwn[::-1].reshape(TAPS, 16, 128).transpose(2, 1, 0))
    cos_f, sin_f = _rope_tables(False)
    cos_r, sin_r = _rope_tables(True)
    in_maps = []
    for c in range(8):
        odd = c % 2 == 1
        xb = x_sample[c // 2]
        if odd:
            xb = np.ascontiguousarray(xb[::-1])
        m = dict(shared)
        m.update(xa=x_prompt[c], xb=xb, cwa=cw_f, cwb=cw_r if odd else cw_f,
                 cosa=cos_f, sina=sin_f, cosb=cos_r if odd else cos_f, sinb=sin_r if odd else sin_f)
        in_maps.append(m)
    return in_maps


def kernel(**inputs):
    in_maps = make_in_maps(**inputs)
    if "nc" not in _NC_CACHE:
        _NC_CACHE["nc"] = build_program()
    nc = _NC_CACHE["nc"]
    res = run_bass_kernel_spmd(nc, in_maps, core_ids=list(range(8)))
    y_prompt = np.empty((8, S, D), np.float32)
    y_sample = np.empty((4, S, D), np.float32)
    for c in range(8):
        r = res.results[c]
        y_prompt[c] = r["ya"]
        yb = np.asarray(r["yb"], np.float32)
        if c % 2 == 0:
            y_sample[c // 2, 0:1024] = yb
        else:
            y_sample[c // 2, 1024:2048] = yb[::-1]
    if DEBUG:
        kernel.last = res
    return (y_prompt, y_sample)
```
